# Optimizing a Trainium2 kernel written in Bass

```python
import jax, jax.numpy as jnp
from jax import lax
import numpy as np

D_MODEL = 2048
BATCH = 4
SEQ = 2048
DEPTH = 2

GRID_W = 64
CTX_LEN = 256
HEAD_DIM = 128
RET_HEADS = D_MODEL // (2 * HEAD_DIM)
RET_DK = HEAD_DIM
RET_DV = HEAD_DIM
RET_CHUNK = 128
MLA_HEADS = D_MODEL // (2 * HEAD_DIM)
MLA_DN = HEAD_DIM
MLA_DR = 64
MLA_DV = HEAD_DIM
MLA_Q_RANK = D_MODEL // 4
MLA_KV_RANK = D_MODEL // 8
MLA_SCALE = (MLA_DN + MLA_DR) ** -0.5
Q_BLOCK = 128
RET_W = RET_HEADS * RET_DK
RET_VW = RET_HEADS * RET_DV
IN_SPLITS = [RET_W, 2 * RET_W, 2 * RET_W + RET_VW, 2 * RET_W + 2 * RET_VW,
             2 * RET_W + 2 * RET_VW + MLA_Q_RANK, 2 * RET_W + 2 * RET_VW + MLA_Q_RANK + MLA_KV_RANK]
IN_COLS = 2 * RET_W + 2 * RET_VW + MLA_Q_RANK + MLA_KV_RANK + MLA_DR
MIX_W = RET_VW + MLA_HEADS * MLA_DV
POOL_WINDOWS = (2, 4, 8, 16)
POOL_G = D_MODEL // len(POOL_WINDOWS)
FFN_DIM = ((8 * D_MODEL // 3 + 255) // 256) * 256
ROPE_BASE = 10000.0
EPS = 1e-6

kernel_name = "hybrid_retention_mla_pool_diffusion_block"


def rmsnorm(x, g):
    x32 = x.astype(jnp.float32)
    y = x32 * lax.rsqrt(jnp.mean(x32 * x32, axis=-1, keepdims=True) + EPS)
    return (y * g).astype(x.dtype)


def modulate(h, shift, scale):
    return h * (1.0 + scale) + shift


def ada_mod(cvec, w, b):
    m = jax.nn.silu(cvec) @ w + b
    return jnp.split(m, 6, axis=-1)


def axial_rope(rows, dim):
    row = jnp.broadcast_to(jnp.arange(rows, dtype=jnp.float32)[:, None], (rows, GRID_W)).reshape(-1)
    col = jnp.broadcast_to(jnp.arange(GRID_W, dtype=jnp.float32)[None, :], (rows, GRID_W)).reshape(-1)
    n_freq = dim // 4
    inv = ROPE_BASE ** (-jnp.arange(n_freq, dtype=jnp.float32) / n_freq)
    ang = jnp.concatenate([row[:, None] * inv, col[:, None] * inv], axis=-1)
    return jnp.cos(ang), jnp.sin(ang)


def apply_rope(x, cos, sin):
    if x.ndim == 4:
        cos, sin = cos[:, None, :], sin[:, None, :]
    half = x.shape[-1] // 2
    x1, x2 = x[..., :half], x[..., half:]
    return jnp.concatenate([x1 * cos - x2 * sin, x1 * sin + x2 * cos], axis=-1).astype(x.dtype)


def retention_scan(q, k, v, log_g, s0, strict):
    B, L, H, _ = q.shape
    n = L // RET_CHUNK

    def chunks(a):
        return a.reshape(B, n, RET_CHUNK, H, a.shape[-1]).transpose(1, 0, 3, 2, 4)

    idx = jnp.arange(RET_CHUNK, dtype=jnp.float32)
    diff = idx[:, None] - idx[None, :]
    mask = (diff > 0) if strict else (diff >= 0)
    decay_in = jnp.where(mask, jnp.exp(jnp.where(mask, diff, 0.0) * log_g[:, None, None]), 0.0)
    q_dec = jnp.exp((idx + 1.0) * log_g[:, None])[..., None]
    k_dec = jnp.exp((RET_CHUNK - 1.0 - idx) * log_g[:, None])[..., None]
    c_dec = jnp.exp(RET_CHUNK * log_g)[:, None, None]

    def step(s, qkv):
        qc, kc, vc = qkv
        inner = jnp.einsum('bhnd,bhmd->bhnm', qc, kc) * decay_in
        o = jnp.einsum('bhnm,bhme->bhne', inner, vc) + jnp.einsum('bhnd,bhde->bhne', qc * q_dec, s)
        s = s * c_dec + jnp.einsum('bhmd,bhme->bhde', kc * k_dec, vc)
        return s, o

    s, o = lax.scan(step, s0, (chunks(q), chunks(k), chunks(v)))
    return s, o.transpose(1, 0, 3, 2, 4).reshape(B, L, H, v.shape[-1])


def bidir_retention(rc, rl, log_f, log_b):
    qc, kc, vc = rc
    ql, kl, vl = rl
    fl = lambda a: jnp.flip(a, axis=1)
    s0 = jnp.zeros((qc.shape[0], RET_HEADS, RET_DK, RET_DV), jnp.float32)
    s_cf, o_cf = retention_scan(qc, kc, vc, log_f, s0, False)
    s_cb, o_cb = retention_scan(fl(qc), fl(kc), fl(vc), log_b, s0, True)
    _, o_lf = retention_scan(ql, kl, vl, log_f, s_cf, False)
    _, o_lb = retention_scan(fl(ql), fl(kl), fl(vl), log_b, s_cb, True)
    return o_cf + fl(o_cb), o_lf + fl(o_lb)


def mla_attention(qn, qr, kn, kr, v):
    B, Lq, H, _ = qn.shape
    nb = Lq // Q_BLOCK

    def to_blocks(a):
        return a.reshape(B, nb, Q_BLOCK, H, a.shape[-1]).swapaxes(0, 1)

    def one_block(qb):
        qn_b, qr_b = qb
        s = jnp.einsum('bqhd,bkhd->bhqk', qn_b, kn) + jnp.einsum('bqhd,bkd->bhqk', qr_b, kr)
        p = jax.nn.softmax(s.astype(jnp.float32) * MLA_SCALE, axis=-1).astype(v.dtype)
        return jnp.einsum('bhqk,bkhd->bqhd', p, v)

    o = lax.map(one_block, (to_blocks(qn), to_blocks(qr)))
    return o.swapaxes(0, 1).reshape(B, Lq, H, v.shape[-1])


def retention_mla_mixer(h_ctx, h_lat, rope_ret, rope_mla, w_in, q_norm_g, w_uq, kv_norm_g, w_ukv,
                        decay_f, decay_b, w_out, with_ctx):
    def project(h, rope_r, rope_m):
        B, L, _ = h.shape
        z = h @ w_in
        rq, rk, rv, rg, cq, ckv, kr = jnp.split(z, IN_SPLITS, axis=-1)
        rq = rq.reshape(B, L, RET_HEADS, RET_DK)
        rk = rk.reshape(B, L, RET_HEADS, RET_DK)
        rv = rv.reshape(B, L, RET_HEADS, RET_DV)
        q = (rmsnorm(cq, q_norm_g) @ w_uq).reshape(B, L, MLA_HEADS, MLA_DN + MLA_DR)
        kv = (rmsnorm(ckv, kv_norm_g) @ w_ukv).reshape(B, L, MLA_HEADS, MLA_DN + MLA_DV)
        qn, qr = q[..., :MLA_DN], q[..., MLA_DN:]
        kn, v = kv[..., :MLA_DN], kv[..., MLA_DN:]
        if rope_r is not None:
            rq, rk = apply_rope(rq, *rope_r), apply_rope(rk, *rope_r)
            qr, kr = apply_rope(qr, *rope_m), apply_rope(kr, *rope_m)
        ret = (rq.astype(jnp.float32), (rk * RET_DK ** -0.5).astype(jnp.float32), rv.astype(jnp.float32))
        return ret, rg, (qn, qr, kn, kr, v)

    ret_c, g_c, mla_c = project(h_ctx, None, None)
    ret_l, g_l, mla_l = project(h_lat, rope_ret, rope_mla)
    log_f = jax.nn.log_sigmoid(decay_f.astype(jnp.float32))
    log_b = jax.nn.log_sigmoid(decay_b.astype(jnp.float32))
    o_ret_c, o_ret_l = bidir_retention(ret_c, ret_l, log_f, log_b)

    def finish(o_ret, gate, o_mla, dtype):
        B, L = o_ret.shape[:2]
        o_ret = o_ret * lax.rsqrt(jnp.mean(o_ret * o_ret, axis=-1, keepdims=True) + EPS)
        o_ret = o_ret.astype(dtype).reshape(B, L, RET_VW) * jax.nn.silu(gate)
        return jnp.concatenate([o_ret, o_mla.reshape(B, L, MLA_HEADS * MLA_DV)], axis=-1) @ w_out

    qn_c, qr_c, kn_c, kr_c, v_c = mla_c
    qn_l, qr_l, kn_l, kr_l, v_l = mla_l
    o_mla_l = mla_attention(qn_l, qr_l, jnp.concatenate([kn_c, kn_l], axis=1),
                            jnp.concatenate([kr_c, kr_l], axis=1), jnp.concatenate([v_c, v_l], axis=1))
    y_lat = finish(o_ret_l, g_l, o_mla_l, h_lat.dtype)
    y_ctx = None
    if with_ctx:
        o_mla_c = mla_attention(qn_c, qr_c, kn_c, kr_c, v_c)
        y_ctx = finish(o_ret_c, g_c, o_mla_c, h_ctx.dtype)
    return y_ctx, y_lat


def multiscale_pool(h, w_pool, scale):
    B, L, D = h.shape
    hf = h.astype(jnp.float32)
    cs = jnp.concatenate([jnp.zeros((B, 1, D), jnp.float32), jnp.cumsum(hf, axis=1)], axis=1)
    t = jnp.arange(L)
    groups = []
    for gi, w in enumerate(POOL_WINDOWS):
        sl = slice(gi * POOL_G, (gi + 1) * POOL_G)
        lo = jnp.clip(t - w // 2, 0, L)
        hi = jnp.clip(t - w // 2 + w, 0, L)
        cnt = (hi - lo).astype(jnp.float32)[:, None]
        groups.append((cs[:, hi, sl] - cs[:, lo, sl]) / cnt - hf[:, :, sl])
    p = jnp.stack(groups, axis=2).astype(h.dtype)
    y = jnp.einsum('blgc,gcd->blgd', p, w_pool).reshape(B, L, D)
    return y * scale


def conv_ffn(h, w_up, conv_w, conv_b, w_down):
    u = h @ w_up
    up = jnp.pad(u, ((0, 0), (1, 1), (0, 0)))
    u = up[:, :-2] * conv_w[0] + up[:, 1:-1] * conv_w[1] + up[:, 2:] * conv_w[2] + conv_b
    a, g = jnp.split(u, 2, axis=-1)
    return (jax.nn.silu(g) * a) @ w_down


def setup_inputs(seed: int = 0) -> dict:
    key = jax.random.key(seed)
    ks = jax.random.split(key, 24)
    n_even = (DEPTH + 1) // 2
    n_odd = DEPTH // 2
    nrm = lambda k, s, sc: jax.random.normal(k, s, jnp.float32) * sc
    base_logit = jnp.asarray(np.log(2.0 ** (5 + np.arange(RET_HEADS)) - 1.0).astype(np.float32))
    return {
        "x": nrm(ks[0], (BATCH, SEQ, D_MODEL), 1.0),
        "c": nrm(ks[1], (BATCH, D_MODEL), 1.0),
        "ctx": nrm(ks[2], (BATCH, CTX_LEN, D_MODEL), 1.0),
        "c_ctx": nrm(ks[3], (D_MODEL,), 1.0),
        "ada_w": nrm(ks[4], (DEPTH, D_MODEL, 6 * D_MODEL), 0.02),
        "ada_b": nrm(ks[5], (DEPTH, 6 * D_MODEL), 0.01),
        "norm1_g": 1.0 + nrm(ks[6], (DEPTH, D_MODEL), 0.02),
        "norm2_g": 1.0 + nrm(ks[7], (DEPTH, D_MODEL), 0.02),
        "ffn_w_up": nrm(ks[8], (DEPTH, D_MODEL, 2 * FFN_DIM), D_MODEL ** -0.5),
        "ffn_conv_w": nrm(ks[9], (DEPTH, 3, 2 * FFN_DIM), 3 ** -0.5),
        "ffn_conv_b": nrm(ks[10], (DEPTH, 2 * FFN_DIM), 0.01),
        "ffn_w_down": nrm(ks[11], (DEPTH, FFN_DIM, D_MODEL), FFN_DIM ** -0.5),
        "mix_w_in": nrm(ks[12], (n_even, D_MODEL, IN_COLS), D_MODEL ** -0.5),
        "mla_q_norm_g": 1.0 + nrm(ks[13], (n_even, MLA_Q_RANK), 0.02),
        "mla_w_uq": nrm(ks[14], (n_even, MLA_Q_RANK, MLA_HEADS * (MLA_DN + MLA_DR)), MLA_Q_RANK ** -0.5),
        "mla_kv_norm_g": 1.0 + nrm(ks[15], (n_even, MLA_KV_RANK), 0.02),
        "mla_w_ukv": nrm(ks[16], (n_even, MLA_KV_RANK, MLA_HEADS * (MLA_DN + MLA_DV)), MLA_KV_RANK ** -0.5),
        "ret_decay_f": base_logit + nrm(ks[17], (n_even, RET_HEADS), 0.1),
        "ret_decay_b": base_logit + nrm(ks[18], (n_even, RET_HEADS), 0.1),
        "mix_w_out": nrm(ks[19], (n_even, MIX_W, D_MODEL), MIX_W ** -0.5),
        "pool_w": nrm(ks[20], (n_odd, len(POOL_WINDOWS), POOL_G, POOL_G), POOL_G ** -0.5),
        "pool_scale": 1.0 + nrm(ks[21], (n_odd, D_MODEL), 0.1),
        "final_g": 1.0 + nrm(ks[22], (D_MODEL,), 0.02),
    }


def reference(x, c, ctx, c_ctx, ada_w, ada_b, norm1_g, norm2_g, ffn_w_up, ffn_conv_w, ffn_conv_b, ffn_w_down,
              mix_w_in, mla_q_norm_g, mla_w_uq, mla_kv_norm_g, mla_w_ukv, ret_decay_f, ret_decay_b, mix_w_out,
              pool_w, pool_scale, final_g):
    rows = x.shape[1] // GRID_W
    rope_ret = axial_rope(rows, RET_DK)
    rope_mla = axial_rope(rows, MLA_DR)
    x_lat, x_ctx = x, ctx
    for l in range(DEPTH):
        j = l // 2
        with_ctx = l < DEPTH - 1
        sh1, sc1, g1, sh2, sc2, g2 = [m[:, None, :] for m in ada_mod(c, ada_w[l], ada_b[l])]
        h_lat = modulate(rmsnorm(x_lat, norm1_g[l]), sh1, sc1)
        if l % 2 == 0 or with_ctx:
            csh1, csc1, cg1, csh2, csc2, cg2 = ada_mod(c_ctx, ada_w[l], ada_b[l])
            h_ctx = modulate(rmsnorm(x_ctx, norm1_g[l]), csh1, csc1)
        if l % 2 == 0:
            o_ctx, o_lat = retention_mla_mixer(h_ctx, h_lat, rope_ret, rope_mla, mix_w_in[j], mla_q_norm_g[j],
                                               mla_w_uq[j], mla_kv_norm_g[j], mla_w_ukv[j], ret_decay_f[j],
                                               ret_decay_b[j], mix_w_out[j], with_ctx)
        else:
            o_lat = multiscale_pool(h_lat, pool_w[j], pool_scale[j])
            o_ctx = multiscale_pool(h_ctx, pool_w[j], pool_scale[j]) if with_ctx else None
        x_lat = x_lat + g1 * o_lat
        x_lat = x_lat + g2 * conv_ffn(modulate(rmsnorm(x_lat, norm2_g[l]), sh2, sc2),
                                      ffn_w_up[l], ffn_conv_w[l], ffn_conv_b[l], ffn_w_down[l])
        if with_ctx:
            x_ctx = x_ctx + cg1 * o_ctx
            x_ctx = x_ctx + cg2 * conv_ffn(modulate(rmsnorm(x_ctx, norm2_g[l]), csh2, csc2),
                                          ffn_w_up[l], ffn_conv_w[l], ffn_conv_b[l], ffn_w_down[l])
    return rmsnorm(x_lat, final_g)
```

```python
import os
import numpy as np
import concourse.bass as bass
import concourse.mybir as mybir
from concourse.bass_utils import run_bass_kernel_spmd

F32 = mybir.dt.float32
BF16 = mybir.dt.bfloat16
AF = mybir.ActivationFunctionType
ALU = mybir.AluOpType

D = 2048
SEQ = 2048
TO = 1024
TH = 1040
NCTX = 256
NT = SEQ + NCTX
FF = 5632
EPS = 1e-6
ENGS = ["pe", "act", "dve", "pool", "sp"]
SHRINK = False
TG = [(0, 347), (347, 694), (694, 1040)]


class Tok:
    __slots__ = ("key", "val")

    def __init__(self, key, val):
        self.key = key
        self.val = val


class Buf:
    def __init__(self, ap, sem=None):
        self.ap = ap
        self.w = None
        self.r = {}
        self.sem = sem

    def __getitem__(self, idx):
        return self.ap[idx]


class Sched:
    def __init__(self, nc):
        self.nc = nc
        self.ops = {e: [] for e in ENGS}
        self.cnt = {}
        self.sems = {}
        self.waited = {e: {} for e in ENGS}
        self._ctx = []
        self.pending_dma = {}
        for e in ENGS:
            self._mk_sem("c_" + e)
        self.nd = 0

    def _mk_sem(self, key):
        cm = self.nc.semaphore(key)
        h = cm.__enter__()
        self._ctx.append(cm)
        self.sems[key] = h
        self.cnt[key] = 0
        return key

    def dsem(self, name=None):
        key = "d_%s" % (name if name is not None else str(self.nd))
        self.nd += 1
        if key not in self.sems:
            self._mk_sem(key)
        return key

    def close(self):
        for cm in reversed(self._ctx):
            cm.__exit__(None, None, None)

    def _waits(self, eng, toks):
        out = []
        w = self.waited[eng]
        for t in toks:
            if t is None:
                continue
            key, val = t.key, t.val
            if key.startswith("d_"):
                val = self.cnt[key]
            if w.get(key, 0) < val:
                w[key] = val
                out.append((key, val))
        return out

    def _deps(self, reads, writes):
        toks = []
        for b in reads:
            toks.append(b.w)
        for b in writes:
            toks.append(b.w)
            for k, v in b.r.items():
                toks.append(Tok(k, v))
        return toks

    def _mark(self, tok, reads, writes):
        for b in reads:
            if b.r.get(tok.key, 0) < tok.val:
                b.r[tok.key] = tok.val
        for b in writes:
            b.w = tok
            b.r = {}

    def run(self, eng, fn, reads=(), writes=()):
        waits = self._waits(eng, self._deps(reads, writes))
        key = "c_" + eng
        self.cnt[key] += 1
        tok = Tok(key, self.cnt[key])
        self.ops[eng].append((fn, waits, (key, 1)))
        self._mark(tok, reads, writes)
        return tok

    def mm(self, fns, reads, writes):
        waits = self._waits("pe", self._deps(reads, writes))
        key = "c_pe"
        self.cnt[key] += 1
        tok = Tok(key, self.cnt[key])
        n = len(fns)
        for i, fn in enumerate(fns):
            self.ops["pe"].append((fn, waits if i == 0 else [], (key, 1) if i == n - 1 else None))
        self._mark(tok, reads, writes)
        return tok

    def dma(self, eng, out, in_, sem, reads=(), writes=(), **kw):
        waits = self._waits(eng, self._deps(reads, writes))
        self.cnt[sem] += 16
        tok = Tok(sem, self.cnt[sem])
        self.ops[eng].append((lambda e: e.dma_start(out=out, in_=in_, **kw), waits, (sem, 16)))
        self._mark(tok, reads, writes)
        self.pending_dma[sem] = True
        return tok

    def _replay(self, ename, e):
        for fn, waits, inc in self.ops[ename]:
            for key, val in waits:
                e.wait_ge(self.sems[key], val)
            if fn is None:
                continue
            ins = fn(e)
            if inc is not None:
                ins.then_inc(self.sems[inc[0]], inc[1])
        self.ops[ename] = []

    def flush(self):
        toks = [Tok(k, self.cnt[k]) for k in self.pending_dma]
        waits = self._waits("sp", toks)
        if waits:
            self.ops["sp"].append((None, waits, None))
        self.pending_dma = {}
        with self.nc.Block() as block:
            @block.tensor
            def _(e):
                self._replay("pe", e)

            @block.scalar
            def _(e):
                self._replay("act", e)

            @block.vector
            def _(e):
                self._replay("dve", e)

            @block.gpsimd
            def _(e):
                self._replay("pool", e)

            @block.sync
            def _(e):
                self._replay("sp", e)


def build_program(debug=False):
    nc = bass.Bass("TRN2", target_bir_lowering=False)
    S = Sched(nc)
    es = []

    def dram(name, shape, dt, kind="ExternalInput"):
        return nc.dram_tensor(name, list(shape), dt, kind=kind).ap()

    def scratch(name, shape, dt):
        return dram(name, shape, dt, kind=("ExternalOutput" if debug else "Internal"))

    uid = [0]

    class Scope:
        def __init__(self):
            self.cms = []

        def sb(self, name, shape, dt):
            uid[0] += 1
            cm = nc.sbuf_tensor("%s_u%d" % (name, uid[0]), list(shape), dt)
            t = cm.__enter__()
            self.cms.append(cm)
            return t

        def ps(self, name, shape, dt):
            cm = nc.psum_tensor(name, list(shape), dt)
            t = cm.__enter__()
            self.cms.append(cm)
            return t

        def close(self):
            for cm in reversed(self.cms):
                cm.__exit__(None, None, None)
            self.cms = []

    x_in = dram("x", [SEQ, D], F32)
    ctx_in = dram("ctx", [NCTX, D], F32)
    cvec = dram("cvec", [128, 16, 2], F32)
    ada_w = dram("ada_w", [2, D, 6 * D], F32)
    ada_b = dram("ada_b", [128, 2, 96], F32)
    vecs = dram("vecs", [128, 5, 16], F32)
    pvec = dram("pvec", [128, 16], F32)
    mlag = dram("mlag", [128, 6], F32)
    convw = dram("convw", [128, 2, 4, 88], F32)
    w_up = dram("w_up", [2, D, 2 * FF], F32)
    w_down = dram("w_down", [2, FF, D], F32)
    w_in = dram("w_in", [D, 4928], F32)
    w_uq = dram("w_uq", [512, 1536], F32)
    w_ukv = dram("w_ukv", [256, 2048], F32)
    w_out = dram("w_out", [D, D], F32)
    pool_w = dram("pool_w", [4, 512, 512], F32)
    dec = dram("dec", [128, 2, 8], F32)
    epos = dram("epos", [128, 4], F32)
    masks = dram("masks", [128, 2, 128], F32)
    ropeR = dram("ropeR", [128, 16, 2, 64], F32)
    ropeM = dram("ropeM", [128, 16, 2, 32], F32)
    pconst = dram("pconst", [128, 4, 16], F32)
    flipd = dram("flipd", [128, 1], F32)
    out_d = dram("out", [TO, D], F32, kind="ExternalOutput")

    xT_d = scratch("xT_d", [16, 128, TH], F32)
    h0T_d = scratch("h0T_d", [16, 128, NT], BF16)
    qT_d = [scratch("q%dT_d" % s, [8, 128, TH], BF16) for s in (1, 2)]
    kT_d = [scratch("k%dT_d" % s, [8, 128, 1152], BF16) for s in (1, 2)]
    k_d = [scratch("k%d_d" % s, [18, 128, 1024], BF16) for s in (1, 2)]
    vr_d = scratch("vr_d", [18, 128, 1024], BF16)
    cqnT_d = scratch("cqnT_d", [4, 128, TH], BF16)
    ckvnT_d = scratch("ckvnT_d", [2, 128, NT], BF16)
    krT_d = scratch("krT_d", [64, NT], BF16)
    sgT_d = scratch("sgT_d", [8, 128, TH], F32)
    mixT_d = scratch("mixT_d", [16, 128, TH], BF16)

    G = Scope()
    ident_f = G.sb("ident_f", [128, 128], F32)
    ident_b = G.sb("ident_b", [128, 128], BF16)
    ones_b = G.sb("ones_b", [128, 128], BF16)
    one_col = G.sb("one_col", [128, 1], F32)
    scv = G.sb("scv", [128, 16, 2], BF16)
    modT = G.sb("modT", [128, 2, 96], F32)
    cmodT = G.sb("cmodT", [128, 32], F32)
    a1 = G.sb("a1", [128, 2, 16], F32)
    a2 = G.sb("a2", [128, 2, 16], F32)
    ca1 = G.sb("ca1", [128, 16], F32)
    gps = G.sb("gps", [128, 16], F32)
    vecs_s = G.sb("vecs_s", [128, 5, 16], F32)
    pvec_s = G.sb("pvec_s", [128, 16], F32)
    mlag_s = G.sb("mlag_s", [128, 6], F32)
    convw_s = G.sb("convw_s", [128, 2, 4, 88], F32)
    adab_s = G.sb("adab_s", [128, 2, 96], F32)
    flip_s = G.sb("flip_s", [128, 1], F32)
    pconst_s = G.sb("pconst_s", [128, 4, 16], F32)
    PF = [G.ps("pf%d" % i, [128, 512], F32) for i in range(6)]
    PB = [G.ps("pb%d" % i, [128, 1024], BF16) for i in range(2)]
    pf = [Buf(t) for t in PF]
    pb = [Buf(t) for t in PB]
    gconst = Buf(ident_f)

    sem_c = S.dsem("const")

    LA = Scope()
    adw = [Buf(LA.sb("adaw%d" % i, [128, 16, 256], BF16), sem=S.dsem("adaw%d" % i)) for i in range(2)]
    L = Scope()
    cv = L.sb("cv", [128, 16, 2], F32)
    identtmp = L.sb("identtmp", [128, 128], F32)
    for dst, src in [(vecs_s, vecs), (pvec_s, pvec), (mlag_s, mlag), (convw_s, convw), (adab_s, ada_b),
                     (flip_s, flipd), (pconst_s, pconst), (cv, cvec)]:
        S.dma("sp", dst[:], src, sem_c, writes=[gconst])
    S.run("pool", lambda e: e.memset(ident_f[:], 0.0), writes=[gconst])
    S.run("pool", lambda e: e.affine_select(out=ident_f[:], in_=ident_f[:], pattern=[[-1, 128]],
                                            compare_op=ALU.not_equal, fill=1.0, base=0, channel_multiplier=1),
          writes=[gconst])
    S.run("dve", lambda e: e.tensor_copy(out=ident_b[:], in_=ident_f[:]), writes=[gconst])
    S.run("dve", lambda e: e.memset(ones_b[:], 1.0), writes=[gconst])
    S.run("dve", lambda e: e.memset(one_col[:], 1.0), writes=[gconst])
    S.run("act", lambda e: e.activation(out=scv[:], in_=cv[:], func=AF.Silu), writes=[gconst])
    pfa = pf[5]
    modB = Buf(modT)
    wv_l = [ada_w[l].rearrange("(c p) n -> p c n", p=128) for l in range(2)]
    ada_state = {"it": 0}
    pav = PF[5][:, 0:4].rearrange("p (f t) -> p f t", t=2)

    def ada_group(l, g2):
        wb = adw[ada_state["it"] % 2]
        ada_state["it"] += 1
        ncol = 2 if (l == 0 and g2 < 16) else 1
        S.dma("pool", wb.ap[:], wv_l[l][:, :, g2 * 256:(g2 + 1) * 256], wb.sem, writes=[wb])
        fns = []
        for j in range(2):
            for c in range(16):
                fns.append(lambda e, wb=wb, j=j, c=c, ncol=ncol: e.matmul(
                    PF[5][:, j * 2:j * 2 + ncol], lhsT=wb.ap[:, c, j * 128:(j + 1) * 128], rhs=scv[:, c, 0:ncol],
                    start=(c == 0), stop=(c == 15)))
        S.mm(fns, reads=[wb, gconst], writes=[pfa])
        fc0 = 2 * g2
        S.run("dve", lambda e, l=l, fc0=fc0: e.tensor_tensor(out=modT[:, l, fc0:fc0 + 2], in0=pav[:, :, 0],
                                                            in1=adab_s[:, l, fc0:fc0 + 2], op=ALU.add), reads=[pfa, gconst], writes=[modB])
        if ncol == 2:
            S.run("dve", lambda e, fc0=fc0: e.tensor_tensor(out=cmodT[:, fc0:fc0 + 2], in0=pav[:, :, 1],
                                                          in1=adab_s[:, 0, fc0:fc0 + 2], op=ALU.add), reads=[pfa, gconst], writes=[modB])

    for g2 in range(16):
        ada_group(0, g2)
    pending = [(0, g2) for g2 in range(16, 48)] + [(1, g2) for g2 in range(48)]

    def bg(n=1):
        for _ in range(n):
            if pending:
                ada_group(*pending.pop(0))

    S.run("dve", lambda e: e.scalar_tensor_tensor(out=a1[:, 0, :], in0=modT[:, 0, 16:32], scalar=1.0,
                                                  in1=vecs_s[:, 0, :], op0=ALU.add, op1=ALU.mult), reads=[modB], writes=[gconst])
    S.run("dve", lambda e: e.scalar_tensor_tensor(out=ca1[:], in0=cmodT[:, 16:32], scalar=1.0,
                                                  in1=vecs_s[:, 0, :], op0=ALU.add, op1=ALU.mult), reads=[modB], writes=[gconst])
    S.flush()
    L.close()

    def derived_consts():
        S.run("dve", lambda e: e.scalar_tensor_tensor(out=a1[:, 1, :], in0=modT[:, 1, 16:32], scalar=1.0,
                                                      in1=vecs_s[:, 1, :], op0=ALU.add, op1=ALU.mult), reads=[modB], writes=[gconst])
        for l in range(2):
            S.run("dve", lambda e, l=l: e.scalar_tensor_tensor(out=a2[:, l, :], in0=modT[:, l, 64:80], scalar=1.0,
                                                                in1=vecs_s[:, 2 + l, :], op0=ALU.add, op1=ALU.mult), reads=[modB], writes=[gconst])
        S.run("dve", lambda e: e.tensor_tensor(out=gps[:], in0=modT[:, 1, 32:48], in1=pvec_s[:], op=ALU.mult), reads=[modB], writes=[gconst])

    def rstd_from_ss(ss_ap, out_ap, n, reads, writes, eng_tmp):
        S.run("dve", lambda e: e.tensor_scalar(out=eng_tmp, in0=ss_ap, scalar1=1.0 / n, scalar2=EPS,
                                               op0=ALU.mult, op1=ALU.add), reads=reads, writes=writes)
        S.run("act", lambda e: e.activation(out=eng_tmp, in_=eng_tmp, func=AF.Sqrt), writes=writes)
        S.run("dve", lambda e: e.reciprocal(out=out_ap, in_=eng_tmp), writes=writes)

    L12 = Scope()
    h0T = L12.sb("h0T", [128, 16, NT], BF16)
    h0Tb = [Buf(h0T[:, :, i * 128:(i + 1) * 128]) for i in range(18)]
    L = Scope()
    xt = [Buf(L.sb("xt%d" % i, [128, D], F32), sem=S.dsem("xt%d" % i)) for i in range(2)]
    xn = [Buf(L.sb("xn%d" % i, [128, D], BF16)) for i in range(2)]
    junk = Buf(L.sb("junk", [128, D], BF16))
    st = [Buf(L.sb("st%d" % i, [128, 4], F32)) for i in range(2)]
    xst = [Buf(L.sb("xst%d" % i, [128, 16, 128], F32), sem=S.dsem("xst%d" % i)) for i in range(2)]
    for i in range(18):
        xb, xnb, sb_, xs_ = xt[i % 2], xn[i % 2], st[i % 2], xst[i % 2]
        src = x_in[i * 128:(i + 1) * 128, :] if i < 16 else ctx_in[(i - 16) * 128:(i - 15) * 128, :]
        S.dma("sp", xb.ap[:], src, xb.sem, writes=[xb])
        S.run("act", lambda e, xb=xb, sb_=sb_: e.activation(out=junk.ap[:], in_=xb.ap[:], func=AF.Square,
                                                            accum_out=sb_.ap[:, 0:1]), reads=[xb], writes=[junk, sb_])
        rstd_from_ss(sb_.ap[:, 0:1], sb_.ap[:, 2:3], D, [sb_], [sb_], sb_.ap[:, 1:2])
        S.run("act", lambda e, xb=xb, xnb=xnb, sb_=sb_: e.activation(out=xnb.ap[:], in_=xb.ap[:], func=AF.Copy,
                                                                     scale=sb_.ap[:, 2:3]), reads=[xb, sb_], writes=[xnb])
        is_ctx = i >= 16
        bg()
        for c in range(16):
            if i <= 8:
                p = pf[c % 2]
                S.mm([lambda e, xb=xb, c=c, p=p: e.transpose(out=p.ap[:, 0:128], in_=xb.ap[:, c * 128:(c + 1) * 128],
                                                             identity=ident_f[:])], reads=[xb, gconst], writes=[p])
                S.run("act", lambda e, c=c, p=p, xs_=xs_: e.activation(
                    out=xs_.ap[:, c, :], in_=p.ap[:, 0:128], func=AF.Copy), reads=[p], writes=[xs_])
            p = pb[c % 2]
            S.mm([lambda e, xnb=xnb, c=c, p=p: e.transpose(out=p.ap[:, 0:128], in_=xnb.ap[:, c * 128:(c + 1) * 128],
                                                           identity=ident_b[:])], reads=[xnb, gconst], writes=[p])
            sc_ap = ca1[:, c:c + 1] if is_ctx else a1[:, 0, c:c + 1]
            sh_ap = cmodT[:, c:c + 1] if is_ctx else modT[:, 0, c:c + 1]
            S.run("dve", lambda e, i=i, c=c, p=p, sc_ap=sc_ap, sh_ap=sh_ap: e.tensor_scalar(
                out=h0T[:, c, i * 128:(i + 1) * 128], in0=p.ap[:, 0:128], scalar1=sc_ap, scalar2=sh_ap,
                op0=ALU.mult, op1=ALU.add), reads=[p, gconst], writes=[h0Tb[i]])
        if i <= 8:
            ncols = 128 if i < 8 else 16
            S.dma("sp", xT_d[:, :, i * 128:i * 128 + ncols].rearrange("c p t -> p c t"), xs_.ap[:, :, 0:ncols],
                  xs_.sem, reads=[xs_])
    if debug == "p1":
        sem = S.dsem("dump")
        S.dma("sp", h0T_d.rearrange("c p t -> p c t"), h0T[:], sem, reads=h0Tb)
    S.flush()
    L.close()
    if debug == "p1":
        L12.close()
        return finish(nc, S, G)

    L = Scope()
    wg = [Buf(L.sb("wg%d" % i, [128, 16, 512], BF16), sem=S.dsem("wg%d" % i)) for i in range(2)]
    ropeR_s = L.sb("ropeR_s", [128, 16, 2, 64], F32)
    ropeM_s = L.sb("ropeM_s", [128, 16, 2, 32], F32)
    dec_s = L.sb("dec_s", [128, 2, 8], F32)
    epos_s = L.sb("epos_s", [128, 4], F32)
    qd = L.sb("qd", [128, 2, 8], F32)
    kd = L.sb("kd", [128, 2, 8], F32)
    tabs = Buf(ropeR_s)
    for dst, src in [(ropeR_s, ropeR), (ropeM_s, ropeM), (dec_s, dec), (epos_s, epos)]:
        S.dma("sp", dst[:], src, sem_c, writes=[tabs])
    SKIP = os.environ.get("P2SKIP", "")
    if "t" not in SKIP:
        S.run("act", lambda e: e.activation(out=dec_s[:], in_=dec_s[:], func=AF.Exp, scale=-1.0), writes=[tabs])
        if "l" not in SKIP:
            S.run("act", lambda e: e.activation(out=dec_s[:], in_=dec_s[:], func=AF.Ln, bias=one_col[:]), reads=[gconst], writes=[tabs])
    for s_ in range(2 if "t" not in SKIP else 0):
        S.run("act", lambda e, s_=s_: e.activation(out=qd[:, s_, :], in_=dec_s[:, s_, :], func=AF.Exp,
                                                   scale=epos_s[:, 2 * s_ + 1:2 * s_ + 2]), writes=[tabs])
        S.run("act", lambda e, s_=s_: e.activation(out=kd[:, s_, :], in_=dec_s[:, s_, :], func=AF.Exp,
                                                   scale=epos_s[:, 2 * s_:2 * s_ + 1]), writes=[tabs])
    S.run("dve", lambda e: e.tensor_scalar(out=kd[:], in0=kd[:], scalar1=float(128 ** -0.5), scalar2=None,
                                           op0=ALU.mult), writes=[tabs])

    w_in_v = w_in.rearrange("(c p) n -> p c n", p=128)
    NR = 3
    rr = [Buf(L.sb("rr%d" % i, [128, 512], F32)) for i in range(NR)]
    tA = [Buf(L.sb("tA%d" % i, [128, 256], F32)) for i in range(NR)]
    tB = [Buf(L.sb("tB%d" % i, [128, 256], F32)) for i in range(NR)]
    ob = [[Buf(L.sb("ob%d_%d" % (s_, i), [128, 512], BF16), sem=S.dsem("ob%d_%d" % (s_, i))) for i in range(NR)]
          for s_ in range(2)]
    tst = [[Buf(L.sb("tst%d_%d" % (s_, i), [128, 4, 128], BF16), sem=S.dsem("tst%d_%d" % (s_, i))) for i in range(NR)]
           for s_ in range(2)]
    sst = [Buf(L.sb("sst%d" % i, [128, 4], F32)) for i in range(NR)]
    sgst = [Buf(L.sb("sgst%d" % i, [128, TH], F32), sem=S.dsem("sgst%d" % i)) for i in range(2)]
    cnt = {"z": 0, "w": 0, "t": 0, "pb": 0}

    def load_wgroup(c0, ncols):
        wb = wg[cnt["w"] % 2]
        cnt["w"] += 1
        if "w" in SKIP:
            return wb
        S.dma("pool", wb.ap[:, :, 0:ncols], w_in_v[:, :, c0:c0 + ncols], wb.sem, writes=[wb])
        return wb

    def proj_tile(wb, ncols, i):
        p = pf[cnt["z"] % 3]
        cnt["z"] += 1
        bg()
        S.mm([lambda e, c=c, p=p, wb=wb: e.matmul(p.ap[:, 0:ncols], lhsT=h0T[:, c, i * 128:(i + 1) * 128],
                                                   rhs=wb.ap[:, c, 0:ncols], start=(c == 0), stop=(c == 15))
              for c in range(16)], reads=[wb, h0Tb[i]], writes=[p])
        return p

    def rope_tm(p, nh, hd, cos_ap, sin_ap, dst, rd=(), tA_=None, tB_=None):
        h2 = hd // 2
        zv = p.ap[:, 0:nh * hd].rearrange("p (h two j) -> p h two j", h=nh, two=2)
        x1, x2 = zv[:, :, 0, :], zv[:, :, 1, :]
        cb = cos_ap.unsqueeze(1).broadcast_to([128, nh, h2])
        sb2 = sin_ap.unsqueeze(1).broadcast_to([128, nh, h2])
        dv = dst.ap[:, 0:nh * hd].rearrange("p (h two j) -> p h two j", h=nh, two=2)
        ta = tA_.ap[:, 0:nh * h2].rearrange("p (h j) -> p h j", h=nh)
        tb = tB_.ap[:, 0:nh * h2].rearrange("p (h j) -> p h j", h=nh)
        S.run("dve", lambda e: e.tensor_tensor(out=ta, in0=x1, in1=cb, op=ALU.mult), reads=[p, tabs], writes=[tA_])
        S.run("dve", lambda e: e.tensor_tensor(out=tb, in0=x2, in1=sb2, op=ALU.mult), reads=[p, tabs], writes=[tB_])
        S.run("dve", lambda e: e.tensor_tensor(out=dv[:, :, 0, :], in0=ta, in1=tb, op=ALU.subtract),
              reads=[tA_, tB_], writes=[dst])
        S.run("dve", lambda e: e.tensor_tensor(out=ta, in0=x1, in1=sb2, op=ALU.mult), reads=[p, tabs], writes=[tA_])
        S.run("dve", lambda e: e.tensor_tensor(out=tb, in0=x2, in1=cb, op=ALU.mult), reads=[p, tabs], writes=[tB_])
        S.run("dve", lambda e: e.tensor_tensor(out=dv[:, :, 1, :], in0=ta, in1=tb, op=ALU.add),
              reads=[tA_, tB_], writes=[dst])

    def transposes_out(src_buf, nblk, blk, dst_dram_ap_fn, npart_out, i, stg, gain_ap_fn=None):
        ncols = 128 if i != 8 else 16
        for j in range(nblk):
            p = pb[cnt["pb"] % 2]
            cnt["pb"] += 1
            S.mm([lambda e, j=j, p=p: e.transpose(out=p.ap[0:blk, 0:128], in_=src_buf.ap[:, j * blk:(j + 1) * blk],
                                                  identity=ident_b[:])], reads=[src_buf, gconst], writes=[p])
            if gain_ap_fn is None:
                S.run("act", lambda e, j=j, p=p: e.activation(out=stg.ap[0:blk, j, :], in_=p.ap[0:blk, 0:128], func=AF.Copy),
                      reads=[p], writes=[stg])
            else:
                S.run("act", lambda e, j=j, p=p: e.activation(out=stg.ap[0:blk, j, :], in_=p.ap[0:blk, 0:128], func=AF.Copy,
                                                              scale=gain_ap_fn(j)), reads=[p, gconst], writes=[stg])
        S.dma("sp", dst_dram_ap_fn(ncols), stg.ap[0:blk, 0:nblk, 0:ncols], stg.sem, reads=[stg])

    LIM = int(os.environ.get("P2LIM", "99"))
    for kind in (("q", "k") if LIM >= 2 else ("q",) if LIM >= 1 else ()):
        for gi in range(2):
            wb = load_wgroup((0 if kind == "q" else 1024) + gi * 512, 512)
            tiles = range(9) if kind == "q" else range(18)
            for i in tiles:
                r = cnt["t"] % NR
                cnt["t"] += 1
                p = proj_tile(wb, 512, i)
                rrb = rr[r]
                if i < 16:
                    rope_tm(p, 4, 128, ropeR_s[:, i, 0, :], ropeR_s[:, i, 1, :], rrb, tA_=tA[r], tB_=tB[r])
                else:
                    S.run("act", lambda e, p=p, rrb=rrb: e.activation(out=rrb.ap[:], in_=p.ap[:], func=AF.Copy),
                          reads=[p], writes=[rrb])
                sc_t = qd if kind == "q" else kd
                for s_ in range(2):
                    o = ob[s_][r]
                    for hh in range(4):
                        S.run("dve" if hh % 2 == 0 else "act", (lambda e, o=o, rrb=rrb, s_=s_, sc_t=sc_t, gi=gi, hh=hh: e.tensor_scalar(
                            out=o.ap[:, hh * 128:(hh + 1) * 128], in0=rrb.ap[:, hh * 128:(hh + 1) * 128],
                            scalar1=sc_t[:, s_, gi * 4 + hh:gi * 4 + hh + 1], scalar2=None, op0=ALU.mult)) if hh % 2 == 0 else
                            (lambda e, o=o, rrb=rrb, s_=s_, sc_t=sc_t, gi=gi, hh=hh: e.activation(
                            out=o.ap[:, hh * 128:(hh + 1) * 128], in_=rrb.ap[:, hh * 128:(hh + 1) * 128], func=AF.Copy,
                            scale=sc_t[:, s_, gi * 4 + hh:gi * 4 + hh + 1])),
                            reads=[rrb, tabs], writes=[o])
                    if kind == "k":
                        S.dma("sp", k_d[s_][i][:, gi * 512:(gi + 1) * 512], o.ap[:], o.sem, reads=[o])
                    if i <= 8:
                        dd = (qT_d if kind == "q" else kT_d)[s_]
                        transposes_out(o, 4, 128, lambda ncols, dd=dd, gi=gi, i=i: dd[gi * 4:gi * 4 + 4, :, i * 128:i * 128 + ncols]
                                       .rearrange("h p t -> p h t"), 128, i if kind == "q" else -1, tst[s_][r])
    for gi in range(2 if LIM >= 3 else 0):
        wb = load_wgroup(2048 + gi * 512, 512)
        for i in range(18):
            r = cnt["t"] % NR
            cnt["t"] += 1
            p = proj_tile(wb, 512, i)
            o = ob[0][r]
            S.run("act", lambda e, p=p, o=o: e.activation(out=o.ap[:], in_=p.ap[:], func=AF.Copy), reads=[p], writes=[o])
            S.dma("sp", vr_d[i][:, gi * 512:(gi + 1) * 512], o.ap[:], o.sem, reads=[o])
    wb = load_wgroup(4096, 512)
    for i in range(9 if LIM >= 4 else 0):
        r = cnt["t"] % NR
        cnt["t"] += 1
        p = proj_tile(wb, 512, i)
        sb_ = sst[r]
        rrb = rr[r]
        S.run("act", lambda e, p=p, sb_=sb_, rrb=rrb: e.activation(out=rrb.ap[:], in_=p.ap[:], func=AF.Square,
                                                                   accum_out=sb_.ap[:, 0:1]), reads=[p], writes=[rrb, sb_])
        rstd_from_ss(sb_.ap[:, 0:1], sb_.ap[:, 2:3], 512, [sb_], [sb_], sb_.ap[:, 1:2])
        o = ob[0][r]
        S.run("act", lambda e, p=p, o=o, sb_=sb_: e.activation(out=o.ap[:], in_=p.ap[:], func=AF.Copy, scale=sb_.ap[:, 2:3]),
              reads=[p, sb_], writes=[o])
        transposes_out(o, 4, 128, lambda ncols, i=i: cqnT_d[:, :, i * 128:i * 128 + ncols].rearrange("c p t -> p c t"),
                       128, i, tst[0][r], gain_ap_fn=lambda j: mlag_s[:, j:j + 1])
    wb = load_wgroup(4608, 320)
    for i in range(18 if LIM >= 5 else 0):
        r = cnt["t"] % NR
        cnt["t"] += 1
        p = proj_tile(wb, 320, i)
        sb_ = sst[r]
        rrb = rr[r]
        S.run("act", lambda e, p=p, sb_=sb_, rrb=rrb: e.activation(out=rrb.ap[:, 0:256], in_=p.ap[:, 0:256], func=AF.Square,
                                                                   accum_out=sb_.ap[:, 0:1]), reads=[p], writes=[rrb, sb_])
        rstd_from_ss(sb_.ap[:, 0:1], sb_.ap[:, 2:3], 256, [sb_], [sb_], sb_.ap[:, 1:2])
        o = ob[0][r]
        S.run("act", lambda e, p=p, o=o, sb_=sb_: e.activation(out=o.ap[:, 0:256], in_=p.ap[:, 0:256], func=AF.Copy,
                                                               scale=sb_.ap[:, 2:3]), reads=[p, sb_], writes=[o])
        transposes_out(o, 2, 128, lambda ncols, i=i: ckvnT_d[:, :, i * 128:i * 128 + ncols].rearrange("c p t -> p c t"),
                       128, -1, tst[0][r], gain_ap_fn=lambda j: mlag_s[:, 4 + j:5 + j])
        o2 = ob[1][r]
        if i < 16:
            pk = Buf(p.ap[:, 256:320])
            pk.w, pk.r = p.w, p.r
            rope_tm(pk, 1, 64, ropeM_s[:, i, 0, :], ropeM_s[:, i, 1, :], rrb, tA_=tA[r], tB_=tB[r])
            p.r.update(pk.r)
            S.run("act", lambda e, o2=o2, rrb=rrb: e.activation(out=o2.ap[:, 0:64], in_=rrb.ap[:, 0:64], func=AF.Copy),
                  reads=[rrb], writes=[o2])
        else:
            S.run("act", lambda e, o2=o2, p=p: e.activation(out=o2.ap[:, 0:64], in_=p.ap[:, 256:320], func=AF.Copy),
                  reads=[p], writes=[o2])
        transposes_out(o2, 1, 64, lambda ncols, i=i: krT_d[:, i * 128:i * 128 + ncols].unsqueeze(1),
                       64, -1, tst[1][r])
    for gi in range(2 if LIM >= 6 else 0):
        wb = load_wgroup(3072 + gi * 512, 512)
        for hh in range(4):
            h = gi * 4 + hh
            sg_ = sgst[h % 2]
            for (t0, t1) in TG:
                p = pf[3 + cnt["z"] % 2]
                cnt["z"] += 1
                S.mm([lambda e, c=c, p=p, wb=wb, hh=hh, t0=t0, t1=t1: e.matmul(
                    p.ap[:, 0:t1 - t0], lhsT=wb.ap[:, c, hh * 128:(hh + 1) * 128], rhs=h0T[:, c, t0:t1],
                    start=(c == 0), stop=(c == 15)) for c in range(16)], reads=[wb] + h0Tb[0:9], writes=[p])
                S.run("act", lambda e, p=p, sg_=sg_, t0=t0, t1=t1: e.activation(out=sg_.ap[:, t0:t1], in_=p.ap[:, 0:t1 - t0],
                                                                             func=AF.Silu), reads=[p], writes=[sg_])
            S.dma("sp", sgT_d[h], sg_.ap[:], sg_.sem, reads=[sg_])
    S.flush()
    L.close()
    L12.close()
    if debug == "p2":
        return finish(nc, S, G)

    rot = {"n": 0}

    def nxt(lst):
        key = tuple(id(b) for b in lst)
        rot[key] = rot.get(key, -1) + 1
        return lst[rot[key] % len(lst)]

    L = Scope()
    dec_s = L.sb("dec3", [128, 2, 8], F32)
    cdt = L.sb("cdt", [128, 2, 8], F32)
    mask_s = L.sb("mask_s", [128, 2, 128], F32)
    Sst = L.sb("Sst", [128, 8, 128], F32)
    Sbf = L.sb("Sbf", [128, 8, 128], BF16)
    Tst = L.sb("Tst", [128, 8, 128], F32)
    oT = L.sb("oT", [128, 8, TH], F32)
    t3 = Buf(dec_s)
    Sb = [Buf(Sst[:, h, :]) for h in range(8)]
    Sbb = [Buf(Sbf[:, h, :]) for h in range(8)]
    Tb = [Buf(Tst[:, 4 * g:4 * g + 4, :]) for g in range(2)]
    oTb = [Buf(oT[:, h, :]) for h in range(8)]
    S.dma("sp", dec_s[:], dec, sem_c, writes=[t3])
    S.dma("sp", mask_s[:], masks, sem_c, writes=[t3])
    S.run("act", lambda e: e.activation(out=dec_s[:], in_=dec_s[:], func=AF.Exp, scale=-1.0), writes=[t3])
    S.run("act", lambda e: e.activation(out=dec_s[:], in_=dec_s[:], func=AF.Ln, bias=one_col[:]), reads=[gconst], writes=[t3])
    S.run("act", lambda e: e.activation(out=cdt[:], in_=dec_s[:], func=AF.Exp, scale=-128.0), writes=[t3])
    kt_ = [Buf(L.sb("kt%d" % i, [128, 1024], BF16), sem=S.dsem("kt%d" % i)) for i in range(2)]
    vt_ = [Buf(L.sb("vt%d" % i, [128, 1024], BF16), sem=S.dsem("vt%d" % i)) for i in range(2)]
    qc_ = [Buf(L.sb("qc%d" % i, [128, 8, 128], BF16), sem=S.dsem("qc%d" % i)) for i in range(2)]
    kc_ = [Buf(L.sb("kc%d" % i, [128, 8, 128], BF16), sem=S.dsem("kc%d" % i)) for i in range(2)]
    At_ = [Buf(L.sb("At%d" % i, [128, 128], BF16)) for i in range(3)]
    ci = 0
    for s_ in range(2):
        order = [16, 17] + list(range(9)) if s_ == 0 else [17, 16] + list(range(15, -1, -1))
        S.run("dve", lambda e: e.memset(Sst[:], 0.0), writes=Sb)
        S.run("dve", lambda e: e.memset(Sbf[:], 0.0), writes=Sbb)
        for oi, c in enumerate(order):
            kt, vt = kt_[ci % 2], vt_[ci % 2]
            qc, kc = qc_[ci % 2], kc_[ci % 2]
            ci += 1
            S.dma("sp", kt.ap[:], k_d[s_][c], kt.sem, writes=[kt])
            S.dma("sp", vt.ap[:], vr_d[c], vt.sem, writes=[vt])
            if c <= 8:
                nq = 128 if c < 8 else 16
                S.dma("sp", qc.ap[:, :, 0:nq], qT_d[s_][:, :, c * 128:c * 128 + nq].rearrange("h p t -> p h t"), qc.sem,
                      writes=[qc])
                S.dma("sp", kc.ap[:], kT_d[s_][:, :, c * 128:(c + 1) * 128].rearrange("h p t -> p h t"), kc.sem, writes=[kc])
                for h in range(8):
                    pi = nxt(pf[0:3])
                    S.mm([lambda e, h=h, pi=pi, kc=kc, qc=qc, nq=nq: e.matmul(pi.ap[:, 0:nq], lhsT=kc.ap[:, h, :], rhs=qc.ap[:, h, 0:nq],
                                                                           start=True, stop=True)], reads=[kc, qc], writes=[pi])
                    At = nxt(At_)
                    S.run("dve", lambda e, pi=pi, At=At, nq=nq, s_=s_: e.tensor_tensor(out=At.ap[:, 0:nq], in0=pi.ap[:, 0:nq],
                                                                                 in1=mask_s[:, s_, 0:nq], op=ALU.mult),
                          reads=[pi, t3], writes=[At])
                    po = nxt(pf[3:5])
                    S.mm([lambda e, h=h, po=po, vt=vt, At=At, nq=nq: e.matmul(po.ap[:, 0:nq], lhsT=vt.ap[:, h * 128:(h + 1) * 128],
                                                                           rhs=At.ap[:, 0:nq], start=True, stop=False),
                          lambda e, h=h, po=po, qc=qc, nq=nq: e.matmul(po.ap[:, 0:nq], lhsT=Sbf[:, h, :], rhs=qc.ap[:, h, 0:nq],
                                                                     start=False, stop=True)],
                         reads=[vt, At, Sbb[h], qc], writes=[po])
                    if s_ == 0:
                        S.run("act", lambda e, h=h, po=po, c=c, nq=nq: e.activation(out=oT[:, h, c * 128:c * 128 + nq], in_=po.ap[:, 0:nq],
                                                                                   func=AF.Copy), reads=[po], writes=[oTb[h]])
                    else:
                        S.run("dve", lambda e, h=h, po=po, c=c, nq=nq: e.tensor_tensor(out=oT[:, h, c * 128:c * 128 + nq], in0=po.ap[:, 0:nq],
                                                                                      in1=oT[:, h, c * 128:c * 128 + nq], op=ALU.add),
                              reads=[po], writes=[oTb[h]])
            if oi == len(order) - 1:
                continue
            for g in range(2):
                pp = nxt(pf[0:3])
                S.mm([lambda e, g=g, hh=hh, pp=pp, kt=kt, vt=vt: e.matmul(
                    pp.ap[:, hh * 128:(hh + 1) * 128], lhsT=kt.ap[:, (4 * g + hh) * 128:(4 * g + hh + 1) * 128],
                    rhs=vt.ap[:, (4 * g + hh) * 128:(4 * g + hh + 1) * 128], start=True, stop=True) for hh in range(4)],
                    reads=[kt, vt], writes=[pp])
                S.run("dve", lambda e, g=g, pp=pp: e.tensor_tensor(out=Tst[:, 4 * g:4 * g + 4, :].rearrange("p h d -> p (h d)"),
                                                                  in0=pp.ap[:, :], in1=Sst[:, 4 * g:4 * g + 4, :].rearrange("p h d -> p (h d)"),
                                                                  op=ALU.add), reads=[pp] + Sb[4 * g:4 * g + 4], writes=[Tb[g]])
                for hh in range(4):
                    h = 4 * g + hh
                    S.run("act", lambda e, h=h, s_=s_: e.activation(out=Sst[:, h, :], in_=Tst[:, h, :], func=AF.Copy,
                                                                   scale=cdt[:, s_, h:h + 1]), reads=[Tb[g], t3], writes=[Sb[h]])
                    S.run("act", lambda e, h=h, s_=s_: e.activation(out=Sbf[:, h, :], in_=Tst[:, h, :], func=AF.Copy,
                                                                   scale=cdt[:, s_, h:h + 1]), reads=[Tb[g], t3], writes=[Sbb[h]])
    if debug == "p3":
        S.dma("sp", sgT_d.rearrange("h p t -> p h t"), oT[:], sem_c, reads=oTb)
    sq_ = [Buf(L.sb("sq%d" % i, [128, TH], BF16)) for i in range(2)]
    rs_ = [Buf(L.sb("rs%d" % i, [128, TH], F32)) for i in range(2)]
    sg_ = [Buf(L.sb("sg%d" % i, [128, TH], F32), sem=S.dsem("sg3_%d" % i)) for i in range(2)]
    mo_ = [Buf(L.sb("mo%d" % i, [128, TH], BF16), sem=S.dsem("mo%d" % i)) for i in range(2)]
    if debug != "p3":
        for h in range(8):
            sq, rs, sg, mo = sq_[h % 2], rs_[h % 2], sg_[h % 2], mo_[h % 2]
            S.dma("sp", sg.ap[:], sgT_d[h], sg.sem, writes=[sg])
            S.run("act", lambda e, h=h, sq=sq: e.activation(out=sq.ap[:], in_=oT[:, h, :], func=AF.Square), reads=[oTb[h]], writes=[sq])
            for (t0, t1) in TG:
                pi = nxt(pf[0:5])
                S.mm([lambda e, pi=pi, sq=sq, t0=t0, t1=t1: e.matmul(pi.ap[:, 0:t1 - t0], lhsT=ones_b[:], rhs=sq.ap[:, t0:t1],
                                                                      start=True, stop=True)], reads=[sq, gconst], writes=[pi])
                S.run("dve", lambda e, pi=pi, rs=rs, t0=t0, t1=t1: e.tensor_scalar(out=rs.ap[:, t0:t1], in0=pi.ap[:, 0:t1 - t0],
                                                                                  scalar1=1.0 / 128, scalar2=EPS, op0=ALU.mult, op1=ALU.add),
                      reads=[pi], writes=[rs])
            S.run("act", lambda e, rs=rs: e.activation(out=rs.ap[:], in_=rs.ap[:], func=AF.Sqrt), writes=[rs])
            S.run("dve", lambda e, rs=rs: e.reciprocal(out=rs.ap[:], in_=rs.ap[:]), writes=[rs])
            S.run("dve", lambda e, rs=rs, h=h: e.tensor_tensor(out=rs.ap[:], in0=rs.ap[:], in1=oT[:, h, :], op=ALU.mult),
                  reads=[oTb[h]], writes=[rs])
            S.run("dve", lambda e, rs=rs, sg=sg, mo=mo: e.tensor_tensor(out=mo.ap[:], in0=rs.ap[:], in1=sg.ap[:], op=ALU.mult),
                  reads=[rs, sg], writes=[mo])
            S.dma("sp", mixT_d[h], mo.ap[:], mo.sem, reads=[mo])
    bg(len(pending))
    S.flush()
    L.close()
    LA.close()
    if debug == "p3":
        return finish(nc, S, G)
    derived_consts()

    L = Scope()
    wuq = Buf(L.sb("wuq", [128, 4, 1536], BF16), sem=S.dsem("wuq"))
    wukv = Buf(L.sb("wukv", [128, 2, 2048], BF16), sem=S.dsem("wukv"))
    cqn = Buf(L.sb("cqn", [128, 4, TH], BF16), sem=S.dsem("cqn"))
    ckvn = Buf(L.sb("ckvn", [128, 2, NT], BF16), sem=S.dsem("ckvn"))
    krT = Buf(L.sb("krT", [128, NT], BF16), sem=S.dsem("krT"))
    ropeM_s = L.sb("ropeM4", [128, 16, 2, 32], F32)
    rM = Buf(ropeM_s, sem=sem_c)
    qnT = L.sb("qnT", [128, 8, TH], BF16)
    qrT = L.sb("qrT", [128, 8, TH], BF16)
    qnTb = [Buf(qnT[:, h, :]) for h in range(8)]
    qrTb = Buf(qrT)
    S.dma("pool", wuq.ap[:], w_uq.rearrange("(c p) n -> p c n", p=128), wuq.sem, writes=[wuq])
    S.dma("pool", wukv.ap[:], w_ukv.rearrange("(c p) n -> p c n", p=128), wukv.sem, writes=[wukv])
    S.dma("sp", cqn.ap[:], cqnT_d.rearrange("c p t -> p c t"), cqn.sem, writes=[cqn])
    S.dma("sp", ckvn.ap[:], ckvnT_d.rearrange("c p t -> p c t"), ckvn.sem, writes=[ckvn])
    S.run("dve", lambda e: e.memset(krT.ap[64:128, :], 0.0), writes=[krT])
    S.run("dve", lambda e: e.memset(qrT[64:128, :, :], 0.0), writes=[qrTb])
    S.dma("sp", krT.ap[0:64, :], krT_d, krT.sem, writes=[krT])
    S.dma("sp", ropeM_s[:], ropeM, sem_c, writes=[rM])
    P4LIM = int(os.environ.get("P4LIM", "99"))
    for h in range(8 if P4LIM >= 1 else 0):
        for (t0, t1) in TG:
            pi = nxt(pf[0:6])
            S.mm([lambda e, c=c, h=h, pi=pi, t0=t0, t1=t1: e.matmul(pi.ap[:, 0:t1 - t0], lhsT=wuq.ap[:, c, h * 192:h * 192 + 128],
                                                                     rhs=cqn.ap[:, c, t0:t1], start=(c == 0), stop=(c == 3))
                  for c in range(4)], reads=[wuq, cqn], writes=[pi])
            S.run("act", lambda e, h=h, pi=pi, t0=t0, t1=t1: e.activation(out=qnT[:, h, t0:t1], in_=pi.ap[:, 0:t1 - t0], func=AF.Copy),
                  reads=[pi], writes=[qnTb[h]])
    rr4 = [Buf(L.sb("rr4_%d" % i, [128, 512], F32)) for i in range(2)]
    ta4 = [Buf(L.sb("ta4_%d" % i, [128, 256], F32)) for i in range(2)]
    tb4 = [Buf(L.sb("tb4_%d" % i, [128, 256], F32)) for i in range(2)]
    qrb = [Buf(L.sb("qrb%d" % i, [128, 512], BF16)) for i in range(2)]
    wq3 = wuq.ap[:].rearrange("p c (h x) -> p c h x", h=8)
    for i in range(9 if P4LIM >= 2 else 0):
        m = 128 if i < 8 else 16
        pi = nxt(pf[0:6])
        S.mm([lambda e, c=c, pi=pi, i=i, m=m: e.matmul(pi.ap[0:m, :], lhsT=cqn.ap[:, c, i * 128:i * 128 + m], rhs=wq3[:, c, :, 128:192],
                                                       start=(c == 0), stop=(c == 3)) for c in range(4)], reads=[wuq, cqn], writes=[pi])
        rrb, ta, tb, qb = rr4[i % 2], ta4[i % 2], tb4[i % 2], qrb[i % 2]
        zv = pi.ap[0:m, :].rearrange("p (h two j) -> p h two j", h=8, two=2)
        x1, x2 = zv[:, :, 0, :], zv[:, :, 1, :]
        cb = ropeM_s[0:m, i, 0, :].unsqueeze(1).broadcast_to([m, 8, 32])
        sb2 = ropeM_s[0:m, i, 1, :].unsqueeze(1).broadcast_to([m, 8, 32])
        dv = rrb.ap[0:m, :].rearrange("p (h two j) -> p h two j", h=8, two=2)
        tav = ta.ap[0:m, :].rearrange("p (h j) -> p h j", h=8)
        tbv = tb.ap[0:m, :].rearrange("p (h j) -> p h j", h=8)
        S.run("dve", lambda e, tav=tav, x1=x1, cb=cb: e.tensor_tensor(out=tav, in0=x1, in1=cb, op=ALU.mult), reads=[pi, rM], writes=[ta])
        S.run("dve", lambda e, tbv=tbv, x2=x2, sb2=sb2: e.tensor_tensor(out=tbv, in0=x2, in1=sb2, op=ALU.mult), reads=[pi, rM], writes=[tb])
        S.run("dve", lambda e, dv=dv, tav=tav, tbv=tbv: e.tensor_tensor(out=dv[:, :, 0, :], in0=tav, in1=tbv, op=ALU.subtract),
              reads=[ta, tb], writes=[rrb])
        S.run("dve", lambda e, tav=tav, x1=x1, sb2=sb2: e.tensor_tensor(out=tav, in0=x1, in1=sb2, op=ALU.mult), reads=[pi, rM], writes=[ta])
        S.run("dve", lambda e, tbv=tbv, x2=x2, cb=cb: e.tensor_tensor(out=tbv, in0=x2, in1=cb, op=ALU.mult), reads=[pi, rM], writes=[tb])
        S.run("dve", lambda e, dv=dv, tav=tav, tbv=tbv: e.tensor_tensor(out=dv[:, :, 1, :], in0=tav, in1=tbv, op=ALU.add),
              reads=[ta, tb], writes=[rrb])
        S.run("act", lambda e, qb=qb, rrb=rrb, m=m: e.activation(out=qb.ap[0:m, :], in_=rrb.ap[0:m, :], func=AF.Copy), reads=[rrb], writes=[qb])
        for h in range(8):
            p = nxt(pb)
            S.mm([lambda e, h=h, p=p, qb=qb, m=m: e.transpose(out=p.ap[0:64, 0:m], in_=qb.ap[0:m, h * 64:(h + 1) * 64],
                                                             identity=ident_b[0:m, 0:m])], reads=[qb, gconst], writes=[p])
            S.run("act", lambda e, h=h, p=p, i=i, m=m: e.activation(out=qrT[0:64, h, i * 128:i * 128 + m], in_=p.ap[0:64, 0:m], func=AF.Copy),
                  reads=[p], writes=[qrTb])
    knT_ = [Buf(L.sb("knT%d" % i, [128, NT], BF16)) for i in range(2)]
    vm_ = [Buf(L.sb("vm%d" % i, [128, 18, 128], BF16)) for i in range(2)]
    PT_ = [Buf(L.sb("PT%d" % i, [128, 512], BF16)) for i in range(3)]
    rd_ = [Buf(L.sb("rd%d" % i, [128, 512], F32)) for i in range(2)]
    os_ = [Buf(L.sb("os%d" % i, [128, TH], BF16), sem=S.dsem("os%d" % i)) for i in range(2)]
    KG = [(0, 512), (512, 1024), (1024, 1536), (1536, 2048), (2048, 2304)]
    scl = float(192 ** -0.5)
    for h in range(8 if P4LIM >= 3 else 0):
        knT, vm, osb = knT_[h % 2], vm_[h % 2], os_[h % 2]
        for (k0, k1) in KG:
            pi = nxt(pf[0:2])
            S.mm([lambda e, c=c, h=h, pi=pi, k0=k0, k1=k1: e.matmul(pi.ap[:, 0:k1 - k0], lhsT=wukv.ap[:, c, h * 256:h * 256 + 128],
                                                                     rhs=ckvn.ap[:, c, k0:k1], start=(c == 0), stop=(c == 1))
                  for c in range(2)], reads=[wukv, ckvn], writes=[pi])
            S.run("act", lambda e, pi=pi, knT=knT, k0=k0, k1=k1: e.activation(out=knT.ap[:, k0:k1], in_=pi.ap[:, 0:k1 - k0], func=AF.Copy),
                  reads=[pi], writes=[knT])
        for k4 in range(5):
            nk = 4 if k4 < 4 else 2
            pi = nxt(pf[0:2])
            fns = []
            for kk in range(nk):
                ktile = k4 * 4 + kk
                for c in range(2):
                    fns.append(lambda e, c=c, h=h, pi=pi, kk=kk, ktile=ktile: e.matmul(
                        pi.ap[:, kk * 128:(kk + 1) * 128], lhsT=ckvn.ap[:, c, ktile * 128:(ktile + 1) * 128],
                        rhs=wukv.ap[:, c, h * 256 + 128:h * 256 + 256], start=(c == 0), stop=(c == 1)))
            S.mm(fns, reads=[wukv, ckvn], writes=[pi])
            S.run("dve", lambda e, pi=pi, vm=vm, k4=k4, nk=nk: e.tensor_copy(
                out=vm.ap[:, k4 * 4:k4 * 4 + nk, :].rearrange("p k d -> p (k d)"), in_=pi.ap[:, 0:nk * 128]), reads=[pi], writes=[vm])
        for (t0, t1) in (TG if P4LIM >= 4 else []):
            n = t1 - t0
            po, pd = nxt(pf[2:4]), nxt(pf[4:6])
            def st_mm(kt, knT=knT, h=h, t0=t0, t1=t1, n=n):
                ps_ = nxt(pf[0:2])
                S.mm([lambda e: e.matmul(ps_.ap[:, 0:n], lhsT=knT.ap[:, kt * 128:(kt + 1) * 128], rhs=qnT[:, h, t0:t1],
                                         start=True, stop=False),
                      lambda e: e.matmul(ps_.ap[:, 0:n], lhsT=krT.ap[:, kt * 128:(kt + 1) * 128], rhs=qrT[:, h, t0:t1],
                                         start=False, stop=True)],
                     reads=[knT, krT, qnTb[h], qrTb], writes=[ps_])
                return ps_
            ps_next = st_mm(0)
            for kt in range(18):
                ps_ = ps_next
                if kt + 1 < 18:
                    ps_next = st_mm(kt + 1)
                PT = nxt(PT_)
                S.run("act", lambda e, ps_=ps_, PT=PT, n=n: e.activation(out=PT.ap[:, 0:n], in_=ps_.ap[:, 0:n], func=AF.Exp, scale=scl),
                      reads=[ps_], writes=[PT])
                S.mm([lambda e, po=po, vm=vm, kt=kt, PT=PT, n=n: e.matmul(po.ap[:, 0:n], lhsT=vm.ap[:, kt, :], rhs=PT.ap[:, 0:n],
                                                                       start=(kt == 0), stop=(kt == 17)),
                      lambda e, pd=pd, kt=kt, PT=PT, n=n: e.matmul(pd.ap[:, 0:n], lhsT=ones_b[:], rhs=PT.ap[:, 0:n],
                                                                 start=(kt == 0), stop=(kt == 17))],
                     reads=[vm, PT, gconst], writes=[po, pd])
            rd = nxt(rd_)
            S.run("dve", lambda e, rd=rd, pd=pd, n=n: e.reciprocal(out=rd.ap[:, 0:n], in_=pd.ap[:, 0:n]), reads=[pd], writes=[rd])
            S.run("dve", lambda e, rd=rd, po=po, osb=osb, t0=t0, t1=t1, n=n: e.tensor_tensor(out=osb.ap[:, t0:t1], in0=po.ap[:, 0:n],
                                                                                         in1=rd.ap[:, 0:n], op=ALU.mult),
                  reads=[po, rd], writes=[osb])
        S.dma("sp", mixT_d[8 + h], osb.ap[:], osb.sem, reads=[osb])
    S.flush()
    L.close()
    if debug == "p4":
        return finish(nc, S, G)

    R = Scope()
    xT = R.sb("xT", [128, 16, TH], F32)
    xTb = [Buf(xT[:, c, :], sem=S.dsem("xTl")) for c in range(16)]
    hT = R.sb("hT", [128, 16, TH], BF16)
    hTb = [Buf(hT[:, c, :]) for c in range(16)]
    rsb = R.sb("rsb", [128, TH], F32)
    rsB = Buf(rsb)
    sqn_ = [Buf(R.sb("sqn%d" % i, [128, TH], BF16)) for i in range(2)]
    tmpn_ = [Buf(R.sb("tmpn%d" % i, [128, TH], F32)) for i in range(2)]
    for c in range(16):
        S.dma("sp", xT[:, c, :], xT_d[c], xTb[c].sem, writes=[xTb[c]])

    def dump_x():
        for c in range(16):
            S.dma("sp", xT_d[c], xT[:, c, :], xTb[c].sem, reads=[xTb[c]])
        S.flush()

    def token_rstd(ncols=TH):
        pss = [pf[3], pf[4], pf[5]]
        for c in range(16):
            sq = sqn_[c % 2]
            S.run("act", lambda e, c=c, sq=sq: e.activation(out=sq.ap[:], in_=xT[:, c, :], func=AF.Square), reads=[xTb[c]], writes=[sq])
            for gi, (t0, t1) in enumerate(TG):
                S.mm([lambda e, gi=gi, sq=sq, t0=t0, t1=t1, c=c: e.matmul(pss[gi].ap[:, 0:t1 - t0], lhsT=ones_b[:], rhs=sq.ap[:, t0:t1],
                                                                          start=(c == 0), stop=(c == 15))], reads=[sq, gconst], writes=[pss[gi]])
        for gi, (t0, t1) in enumerate(TG):
            S.run("dve", lambda e, gi=gi, t0=t0, t1=t1: e.tensor_scalar(out=rsb[:, t0:t1], in0=pss[gi].ap[:, 0:t1 - t0], scalar1=1.0 / D,
                                                                        scalar2=EPS, op0=ALU.mult, op1=ALU.add), reads=[pss[gi]], writes=[rsB])
        S.run("act", lambda e: e.activation(out=rsb[:], in_=rsb[:], func=AF.Sqrt), writes=[rsB])
        S.run("dve", lambda e: e.reciprocal(out=rsb[:], in_=rsb[:]), writes=[rsB])

    def prenorm(a_fn, sh_fn):
        token_rstd()
        for c in range(16):
            tm_ = tmpn_[c % 2]
            S.run("dve", lambda e, c=c, tm_=tm_: e.scalar_tensor_tensor(out=tm_.ap[:], in0=xT[:, c, :], scalar=a_fn(c), in1=rsb[:],
                                                                        op0=ALU.mult, op1=ALU.mult), reads=[xTb[c], rsB, gconst], writes=[tm_])
            S.run("act", lambda e, c=c, tm_=tm_: e.activation(out=hT[:, c, :], in_=tm_.ap[:], func=AF.Identity, bias=sh_fn(c)),
                  reads=[tm_, gconst], writes=[hTb[c]])

    L = Scope()
    mixT = L.sb("mixT", [128, 16, TH], BF16)
    mixB = Buf(mixT, sem=S.dsem("mixT"))
    S.dma("sp", mixT[:], mixT_d.rearrange("c p t -> p c t"), mixB.sem, writes=[mixB])
    wo_ = [Buf(L.sb("wo%d" % i, [128, 16, 512], BF16), sem=S.dsem("wo%d" % i)) for i in range(2)]
    w_out_v = w_out.rearrange("(c p) n -> p c n", p=128)
    for g4 in range(4):
        wo = wo_[g4 % 2]
        S.dma("pool", wo.ap[:], w_out_v[:, :, g4 * 512:(g4 + 1) * 512], wo.sem, writes=[wo])
        for dl in range(4):
            dc = g4 * 4 + dl
            for (t0, t1) in TG:
                pi = nxt(pf[0:3])
                S.mm([lambda e, c=c, pi=pi, wo=wo, dl=dl, t0=t0, t1=t1: e.matmul(pi.ap[:, 0:t1 - t0], lhsT=wo.ap[:, c, dl * 128:(dl + 1) * 128],
                                                                                 rhs=mixT[:, c, t0:t1], start=(c == 0), stop=(c == 15))
                      for c in range(16)], reads=[wo, mixB], writes=[pi])
                S.run("dve", lambda e, pi=pi, dc=dc, t0=t0, t1=t1: e.scalar_tensor_tensor(
                    out=xT[:, dc, t0:t1], in0=pi.ap[:, 0:t1 - t0], scalar=modT[:, 0, 32 + dc:33 + dc], in1=xT[:, dc, t0:t1],
                    op0=ALU.mult, op1=ALU.add), reads=[pi, gconst], writes=[xTb[dc]])
    S.flush()
    L.close()
    if debug == "p5":
        dump_x()
        R.close()
        return finish(nc, S, G)

    def ffn(l):
        prenorm(lambda c: a2[:, l, c:c + 1], lambda c: modT[:, l, 48 + c:49 + c])
        L = Scope()
        wu_ = [Buf(L.sb("wu%d" % i, [128, 16, 512], BF16), sem=S.dsem("wu%d" % i)) for i in range(2)]
        wd_ = [Buf(L.sb("wd%d" % i, [128, 2, 2048], BF16), sem=S.dsem("wd%d" % i)) for i in range(2)]
        act_ = [Buf(L.sb("actT%d" % i, [128, 2, TH], BF16)) for i in range(2)]
        ua_ = [Buf(L.sb("ua%d" % i, [128, 352], F32)) for i in range(2)]
        ug_ = [Buf(L.sb("ug%d" % i, [128, 352], F32)) for i in range(2)]
        ca_ = [Buf(L.sb("ca%d" % i, [128, 352], F32)) for i in range(2)]
        cg_ = [Buf(L.sb("cg%d" % i, [128, 352], F32)) for i in range(2)]
        wuv = w_up[l].rearrange("(c p) n -> p c n", p=128)
        wdv = w_down[l].rearrange("(j p) n -> p j n", p=128)
        uic = [0]

        def U(fb, dgen=None):
            wu, wd, actT = wu_[fb % 2], wd_[fb % 2], act_[fb % 2]
            S.dma("pool", wu.ap[:, :, 0:256], wuv[:, :, fb * 256:(fb + 1) * 256], wu.sem, writes=[wu])
            S.dma("pool", wu.ap[:, :, 256:512], wuv[:, :, FF + fb * 256:FF + (fb + 1) * 256], wu.sem, writes=[wu])
            S.dma("pool", wd.ap[:], wdv[:, fb * 2:fb * 2 + 2, :], wd.sem, writes=[wd])
            for j in range(2):
                jj = fb * 2 + j
                for gi, (t0, t1) in enumerate(TG):
                    lo, hi = max(t0 - 1, 0), min(t1 + 1, TH)
                    n, nn = t1 - t0, hi - lo
                    off = lo - (t0 - 1)
                    ua, ug, ca, cg = ua_[uic[0] % 2], ug_[uic[0] % 2], ca_[uic[0] % 2], cg_[uic[0] % 2]
                    uic[0] += 1
                    for (ub, col0, pp) in ((ua, j * 128, nxt(pf[0:2])), (ug, 256 + j * 128, nxt(pf[2:4]))):
                        S.mm([lambda e, c=c, pp=pp, wu=wu, col0=col0, lo=lo, hi=hi, nn=nn: e.matmul(
                            pp.ap[:, 0:nn], lhsT=wu.ap[:, c, col0:col0 + 128], rhs=hT[:, c, lo:hi], start=(c == 0), stop=(c == 15))
                            for c in range(16)], reads=[wu] + hTb, writes=[pp])
                        if off == 1:
                            S.run("dve", lambda e, ub=ub: e.memset(ub.ap[:, 0:1], 0.0), writes=[ub])
                        if hi == TH:
                            S.run("dve", lambda e, ub=ub, nn=nn, off=off: e.memset(ub.ap[:, off + nn:off + nn + 1], 0.0), writes=[ub])
                        S.run("act", lambda e, ub=ub, pp=pp, nn=nn, off=off: e.activation(out=ub.ap[:, off:off + nn], in_=pp.ap[:, 0:nn],
                                                                                       func=AF.Copy), reads=[pp], writes=[ub])
                        cb_, ch = (ca, jj) if ub is ua else (cg, 44 + jj)
                        S.run("act", lambda e, cb_=cb_, pp=pp, n=n, off=off, ch=ch: e.activation(
                            out=cb_.ap[:, 0:n], in_=pp.ap[:, 1 - off:1 - off + n], func=AF.Identity,
                            scale=convw_s[:, l, 1, ch:ch + 1], bias=convw_s[:, l, 3, ch:ch + 1]), reads=[pp, gconst], writes=[cb_])
                    for (ub, cb_, ch) in ((ua, ca, jj), (ug, cg, 44 + jj)):
                        S.run("dve", lambda e, ub=ub, cb_=cb_, ch=ch, n=n: e.scalar_tensor_tensor(
                            out=cb_.ap[:, 0:n], in0=ub.ap[:, 0:n], scalar=convw_s[:, l, 0, ch:ch + 1], in1=cb_.ap[:, 0:n],
                            op0=ALU.mult, op1=ALU.add), reads=[ub, gconst], writes=[cb_])
                        S.run("dve", lambda e, ub=ub, cb_=cb_, ch=ch, n=n: e.scalar_tensor_tensor(
                            out=cb_.ap[:, 0:n], in0=ub.ap[:, 2:n + 2], scalar=convw_s[:, l, 2, ch:ch + 1], in1=cb_.ap[:, 0:n],
                            op0=ALU.mult, op1=ALU.add), reads=[ub, gconst], writes=[cb_])
                    S.run("act", lambda e, cg=cg, n=n: e.activation(out=cg.ap[:, 0:n], in_=cg.ap[:, 0:n], func=AF.Silu), writes=[cg])
                    S.run("dve", lambda e, cg=cg, ca=ca, actT=actT, j=j, t0=t0, t1=t1, n=n: e.tensor_tensor(
                        out=actT.ap[:, j, t0:t1], in0=cg.ap[:, 0:n], in1=ca.ap[:, 0:n], op=ALU.mult), reads=[cg, ca], writes=[actT])
                    if dgen is not None:
                        for _ in range(8):
                            next(dgen, None)

        def Dp(fb):
            wd, actT = wd_[fb % 2], act_[fb % 2]
            for dc in range(16):
                for (t0, t1) in TG:
                    pi = nxt(pf[4:6])
                    S.mm([lambda e, j=j, pi=pi, wd=wd, dc=dc, actT=actT, t0=t0, t1=t1: e.matmul(
                        pi.ap[:, 0:t1 - t0], lhsT=wd.ap[:, j, dc * 128:(dc + 1) * 128], rhs=actT.ap[:, j, t0:t1], start=(j == 0), stop=(j == 1))
                        for j in range(2)], reads=[wd, actT], writes=[pi])
                    S.run("dve", lambda e, pi=pi, dc=dc, t0=t0, t1=t1: e.scalar_tensor_tensor(
                        out=xT[:, dc, t0:t1], in0=pi.ap[:, 0:t1 - t0], scalar=modT[:, l, 80 + dc:81 + dc], in1=xT[:, dc, t0:t1],
                        op0=ALU.mult, op1=ALU.add), reads=[pi, gconst], writes=[xTb[dc]])
                    yield

        U(0)
        for fb in range(22):
            dgen = Dp(fb)
            if fb + 1 < 22:
                U(fb + 1, dgen)
            for _ in dgen:
                pass
        S.flush()
        L.close()

    ffn(0)
    if debug == "p7":
        dump_x()
        R.close()
        return finish(nc, S, G)

    token_rstd()
    L = Scope()
    PW = 8 + TH + 16
    hp_ = [Buf(L.sb("hp%d" % i, [128, PW], F32)) for i in range(2)]
    wa_ = [Buf(L.sb("wa%d" % i, [128, PW], F32)) for i in range(2)]
    wb_ = [Buf(L.sb("wb%d" % i, [128, PW], F32)) for i in range(2)]
    t16_ = [Buf(L.sb("t16_%d" % i, [128, 16], F32)) for i in range(2)]
    pw_ = [Buf(L.sb("pw%d" % i, [128, 4, 512], BF16), sem=S.dsem("pw%d" % i)) for i in range(2)]
    for hb in hp_:
        S.run("dve", lambda e, hb=hb: e.memset(hb.ap[:], 0.0), writes=[hb])
    for g in range(4):
        w = 2 << g
        pw = pw_[g % 2]
        S.dma("pool", pw.ap[:], pool_w[g].rearrange("(c p) n -> p c n", p=128), pw.sem, writes=[pw])
        for cc in range(4):
            c = 4 * g + cc
            hb, wa, wb2, t16 = hp_[c % 2], wa_[c % 2], wb_[c % 2], t16_[c % 2]
            S.run("dve", lambda e, c=c, hb=hb: e.scalar_tensor_tensor(out=hb.ap[:, 8:8 + TH], in0=xT[:, c, :], scalar=a1[:, 1, c:c + 1],
                                                                      in1=rsb[:], op0=ALU.mult, op1=ALU.mult),
                  reads=[xTb[c], rsB, gconst], writes=[hb])
            S.run("act", lambda e, c=c, hb=hb: e.activation(out=hb.ap[:, 8:8 + TH], in_=hb.ap[:, 8:8 + TH], func=AF.Identity,
                                                            bias=modT[:, 1, c:c + 1]), reads=[gconst], writes=[hb])
            src, width, k = hb, PW, 1
            bufs = [wa, wb2]
            bi = 0
            while k < w:
                dst = bufs[bi % 2]
                bi += 1
                nw = width - k
                S.run("dve", lambda e, src=src, dst=dst, nw=nw, k=k: e.tensor_tensor(out=dst.ap[:, 0:nw], in0=src.ap[:, 0:nw],
                                                                                   in1=src.ap[:, k:k + nw], op=ALU.add),
                      reads=[src], writes=[dst])
                src, width, k = dst, nw, 2 * k
            s0 = 8 - w // 2
            oth = bufs[bi % 2]
            S.run("dve", lambda e, src=src, oth=oth, s0=s0: e.tensor_tensor(out=oth.ap[:, 0:TH], in0=src.ap[:, s0 + 1:s0 + 1 + TH],
                                                                          in1=src.ap[:, s0:s0 + TH], op=ALU.subtract),
                  reads=[src], writes=[oth])
            S.run("dve", lambda e, src=src, oth=oth, s0=s0: e.scalar_tensor_tensor(out=oth.ap[:, 0:TH], in0=oth.ap[:, 0:TH], scalar=flip_s[:, 0:1],
                                                                                 in1=src.ap[:, s0:s0 + TH], op0=ALU.mult, op1=ALU.add),
                  reads=[src, gconst], writes=[oth])
            S.run("dve", lambda e, oth=oth, g=g, t16=t16: e.tensor_tensor(out=t16.ap[:], in0=oth.ap[:, 0:16], in1=pconst_s[:, g, :], op=ALU.mult),
                  reads=[oth, gconst], writes=[t16])
            S.run("dve", lambda e, oth=oth, hb=hb, c=c, w=w: e.scalar_tensor_tensor(out=hT[:, c, :], in0=oth.ap[:, 0:TH], scalar=1.0 / w,
                                                                                  in1=hb.ap[:, 8:8 + TH], op0=ALU.mult, op1=ALU.subtract),
                  reads=[oth, hb], writes=[hTb[c]])
            S.run("dve", lambda e, hb=hb, c=c, t16=t16: e.tensor_tensor(out=hT[:, c, 0:16], in0=t16.ap[:], in1=hb.ap[:, 8:24], op=ALU.subtract),
                  reads=[t16, hb], writes=[hTb[c]])
        for dl in range(4):
            dc = 4 * g + dl
            for (t0, t1) in TG:
                pi = nxt(pf[0:3])
                S.mm([lambda e, cc=cc, pi=pi, pw=pw, dl=dl, g=g, t0=t0, t1=t1: e.matmul(
                    pi.ap[:, 0:t1 - t0], lhsT=pw.ap[:, cc, dl * 128:(dl + 1) * 128], rhs=hT[:, 4 * g + cc, t0:t1], start=(cc == 0), stop=(cc == 3))
                    for cc in range(4)], reads=[pw] + hTb[4 * g:4 * g + 4], writes=[pi])
                S.run("dve", lambda e, pi=pi, dc=dc, t0=t0, t1=t1: e.scalar_tensor_tensor(
                    out=xT[:, dc, t0:t1], in0=pi.ap[:, 0:t1 - t0], scalar=gps[:, dc:dc + 1], in1=xT[:, dc, t0:t1],
                    op0=ALU.mult, op1=ALU.add), reads=[pi, gconst], writes=[xTb[dc]])
    S.flush()
    L.close()
    if debug == "p8":
        dump_x()
        R.close()
        return finish(nc, S, G)

    ffn(1)

    token_rstd()
    L = Scope()
    yT = L.sb("yT", [128, 16, TO], F32)
    yTb = [Buf(yT[:, c, :]) for c in range(16)]
    ot_ = [Buf(L.sb("ot%d" % i, [128, D], F32), sem=S.dsem("ot%d" % i)) for i in range(2)]
    for c in range(16):
        S.run("dve", lambda e, c=c: e.scalar_tensor_tensor(out=yT[:, c, :], in0=xT[:, c, 0:TO], scalar=vecs_s[:, 4, c:c + 1], in1=rsb[:, 0:TO],
                                                           op0=ALU.mult, op1=ALU.mult), reads=[xTb[c], rsB, gconst], writes=[yTb[c]])
    for i in range(8):
        ot = ot_[i % 2]
        for c in range(16):
            pi = nxt(pf[0:6])
            S.mm([lambda e, c=c, i=i, pi=pi: e.transpose(out=pi.ap[:, 0:128], in_=yT[:, c, i * 128:(i + 1) * 128], identity=ident_f[:])],
                 reads=[yTb[c], gconst], writes=[pi])
            S.run("act" if c % 2 == 0 else "dve", (lambda e, c=c, pi=pi, ot=ot: e.activation(out=ot.ap[:, c * 128:(c + 1) * 128], in_=pi.ap[:, 0:128],
                                                                                        func=AF.Copy)) if c % 2 == 0 else
                  (lambda e, c=c, pi=pi, ot=ot: e.tensor_copy(out=ot.ap[:, c * 128:(c + 1) * 128], in_=pi.ap[:, 0:128])),
                  reads=[pi], writes=[ot])
        S.dma("sp", out_d[i * 128:(i + 1) * 128, :], ot.ap[:], ot.sem, reads=[ot])
    S.flush()
    L.close()
    R.close()
    return finish(nc, S, G)


def finish(nc, S, G):
    G.close()
    S.close()
    return nc


def _pp(v):
    v = np.asarray(v, np.float32)
    return np.ascontiguousarray(v.reshape(-1, 128).T)


def _rope_tab(flip, dim):
    j = np.arange(SEQ)
    t = (SEQ - 1 - j) if flip else j
    row = (t // 64).astype(np.float32)
    col = (t % 64).astype(np.float32)
    nf = dim // 4
    inv = (10000.0 ** (-np.arange(nf, dtype=np.float32) / nf)).astype(np.float32)
    ang = np.concatenate([row[:, None] * inv, col[:, None] * inv], -1)
    cs = np.stack([np.cos(ang), np.sin(ang)], 1).astype(np.float32)
    return np.ascontiguousarray(cs.reshape(16, 128, 2, dim // 2).transpose(1, 0, 2, 3))


def prepare_inputs(inp, k):
    b, half = k // 2, k % 2
    flip = half == 1
    x = inp["x"][b]
    ctx = inp["ctx"][b]
    if flip:
        x = x[::-1]
        ctx = ctx[::-1]
    m = {}
    m["x"] = np.ascontiguousarray(x, np.float32)
    m["ctx"] = np.ascontiguousarray(ctx, np.float32)
    m["cvec"] = np.ascontiguousarray(np.stack([_pp(inp["c"][b]), _pp(inp["c_ctx"])], -1))
    m["ada_w"] = inp["ada_w"]
    m["ada_b"] = np.ascontiguousarray(np.stack([_pp(inp["ada_b"][0]), _pp(inp["ada_b"][1])], 1))
    m["vecs"] = np.ascontiguousarray(np.stack([_pp(inp["norm1_g"][0]), _pp(inp["norm1_g"][1]), _pp(inp["norm2_g"][0]),
                                               _pp(inp["norm2_g"][1]), _pp(inp["final_g"])], 1))
    m["pvec"] = _pp(inp["pool_scale"][0])
    m["mlag"] = np.ascontiguousarray(np.concatenate([_pp(inp["mla_q_norm_g"][0]), _pp(inp["mla_kv_norm_g"][0])], 1))
    cw = np.zeros((128, 2, 4, 88), np.float32)
    for l in range(2):
        w3 = inp["ffn_conv_w"][l]
        if flip:
            w3 = w3[::-1]
        for t in range(3):
            cw[:, l, t, :] = _pp(w3[t])
        cw[:, l, 3, :] = _pp(inp["ffn_conv_b"][l])
    m["convw"] = cw
    m["w_up"] = inp["ffn_w_up"]
    m["w_down"] = inp["ffn_w_down"]
    m["w_in"] = inp["mix_w_in"][0]
    m["w_uq"] = inp["mla_w_uq"][0]
    m["w_ukv"] = inp["mla_w_ukv"][0]
    m["w_out"] = inp["mix_w_out"][0]
    m["pool_w"] = inp["pool_w"][0]
    df, db = inp["ret_decay_f"][0], inp["ret_decay_b"][0]
    d12 = np.stack([db, df], 0) if flip else np.stack([df, db], 0)
    m["dec"] = np.ascontiguousarray(np.broadcast_to(d12[None], (128, 2, 8)), np.float32)
    p = np.arange(128, dtype=np.float32)
    m["epos"] = np.ascontiguousarray(np.stack([p + 1, -(p + 1), 128 - p, -(128 - p)], 1), np.float32)
    mi, ni = np.meshgrid(np.arange(128), np.arange(128), indexing="ij")
    strict1, strict2 = (True, False) if flip else (False, True)
    mk1 = (ni > mi) if strict1 else (ni >= mi)
    mk2 = (ni < mi) if strict2 else (ni <= mi)
    m["masks"] = np.ascontiguousarray(np.stack([mk1, mk2], 1).astype(np.float32))
    m["ropeR"] = _rope_tab(flip, 128)
    m["ropeM"] = _rope_tab(flip, 64)
    j = np.arange(16)
    t = (SEQ - 1 - j) if flip else j
    pc = np.zeros((4, 16), np.float32)
    for gi, w in enumerate((2, 4, 8, 16)):
        lo = np.clip(t - w // 2, 0, SEQ)
        hi = np.clip(t - w // 2 + w, 0, SEQ)
        pc[gi] = 1.0 / (hi - lo)
    m["pconst"] = np.ascontiguousarray(np.broadcast_to(pc[None], (128, 4, 16)), np.float32)
    m["flipd"] = np.full((128, 1), 1.0 if flip else 0.0, np.float32)
    return m


_NC_CACHE = {}


def kernel(**inputs):
    inp = {k: np.asarray(v) for k, v in inputs.items()}
    if "nc" not in _NC_CACHE:
        _NC_CACHE["nc"] = build_program()
    nc = _NC_CACHE["nc"]
    in_maps = [prepare_inputs(inp, k) for k in range(8)]
    res = run_bass_kernel_spmd(nc, in_maps, core_ids=list(range(8)))
    out = np.zeros((4, SEQ, D), np.float32)
    for k in range(8):
        o = np.asarray(res.results[k]["out"], np.float32)
        b, half = k // 2, k % 2
        if half == 0:
            out[b, :TO] = o
        else:
            out[b, TO:] = o[::-1]
    return out
```

```python
import os
import numpy as np
import concourse.bass as bass
import concourse.mybir as mybir
from concourse.bass_utils import run_bass_kernel_spmd

F32 = mybir.dt.float32
BF16 = mybir.dt.bfloat16
AF = mybir.ActivationFunctionType
ALU = mybir.AluOpType

D = 2048
SEQ = 2048
TO = 1024
TH = 1040
NCTX = 256
NT = SEQ + NCTX
FF = 5632
EPS = 1e-6
ENGS = ["pe", "act", "dve", "pool", "sp"]
SHRINK = False
TG = [(0, 347), (347, 694), (694, 1040)]


class Tok:
    __slots__ = ("key", "val")

    def __init__(self, key, val):
        self.key = key
        self.val = val


class Buf:
    def __init__(self, ap, sem=None):
        self.ap = ap
        self.w = None
        self.r = {}
        self.sem = sem

    def __getitem__(self, idx):
        return self.ap[idx]


class Sched:
    def __init__(self, nc):
        self.nc = nc
        self.ops = {e: [] for e in ENGS}
        self.cnt = {}
        self.sems = {}
        self.waited = {e: {} for e in ENGS}
        self._ctx = []
        self.pending_dma = {}
        for e in ENGS:
            self._mk_sem("c_" + e)
        self.nd = 0

    def _mk_sem(self, key):
        cm = self.nc.semaphore(key)
        h = cm.__enter__()
        self._ctx.append(cm)
        self.sems[key] = h
        self.cnt[key] = 0
        return key

    def dsem(self, name=None):
        key = "d_%s" % (name if name is not None else str(self.nd))
        self.nd += 1
        if key not in self.sems:
            self._mk_sem(key)
        return key

    def close(self):
        for cm in reversed(self._ctx):
            cm.__exit__(None, None, None)

    def _waits(self, eng, toks):
        out = []
        w = self.waited[eng]
        for t in toks:
            if t is None:
                continue
            key, val = t.key, t.val
            if key.startswith("d_"):
                val = self.cnt[key]
            if w.get(key, 0) < val:
                w[key] = val
                out.append((key, val))
        return out

    def _deps(self, reads, writes):
        toks = []
        for b in reads:
            toks.append(b.w)
        for b in writes:
            toks.append(b.w)
            for k, v in b.r.items():
                toks.append(Tok(k, v))
        return toks

    def _mark(self, tok, reads, writes):
        for b in reads:
            if b.r.get(tok.key, 0) < tok.val:
                b.r[tok.key] = tok.val
        for b in writes:
            b.w = tok
            b.r = {}

    def run(self, eng, fn, reads=(), writes=()):
        waits = self._waits(eng, self._deps(reads, writes))
        key = "c_" + eng
        self.cnt[key] += 1
        tok = Tok(key, self.cnt[key])
        self.ops[eng].append((fn, waits, (key, 1)))
        self._mark(tok, reads, writes)
        return tok

    def mm(self, fns, reads, writes):
        waits = self._waits("pe", self._deps(reads, writes))
        key = "c_pe"
        self.cnt[key] += 1
        tok = Tok(key, self.cnt[key])
        n = len(fns)
        for i, fn in enumerate(fns):
            self.ops["pe"].append((fn, waits if i == 0 else [], (key, 1) if i == n - 1 else None))
        self._mark(tok, reads, writes)
        return tok

    def dma(self, eng, out, in_, sem, reads=(), writes=(), **kw):
        waits = self._waits(eng, self._deps(reads, writes))
        self.cnt[sem] += 16
        tok = Tok(sem, self.cnt[sem])
        self.ops[eng].append((lambda e: e.dma_start(out=out, in_=in_, **kw), waits, (sem, 16)))
        self._mark(tok, reads, writes)
        self.pending_dma[sem] = True
        return tok

    def _replay(self, ename, e):
        for fn, waits, inc in self.ops[ename]:
            for key, val in waits:
                e.wait_ge(self.sems[key], val)
            if fn is None:
                continue
            ins = fn(e)
            if inc is not None:
                ins.then_inc(self.sems[inc[0]], inc[1])
        self.ops[ename] = []

    def flush(self):
        toks = [Tok(k, self.cnt[k]) for k in self.pending_dma]
        waits = self._waits("sp", toks)
        if waits:
            self.ops["sp"].append((None, waits, None))
        self.pending_dma = {}
        with self.nc.Block() as block:
            @block.tensor
            def _(e):
                self._replay("pe", e)

            @block.scalar
            def _(e):
                self._replay("act", e)

            @block.vector
            def _(e):
                self._replay("dve", e)

            @block.gpsimd
            def _(e):
                self._replay("pool", e)

            @block.sync
            def _(e):
                self._replay("sp", e)


def build_program(debug=False):
    nc = bass.Bass("TRN2", target_bir_lowering=False)
    S = Sched(nc)
    es = []

    def dram(name, shape, dt, kind="ExternalInput"):
        return nc.dram_tensor(name, list(shape), dt, kind=kind).ap()

    def scratch(name, shape, dt):
        return dram(name, shape, dt, kind=("ExternalOutput" if debug else "Internal"))

    uid = [0]

    class Scope:
        def __init__(self):
            self.cms = []

        def sb(self, name, shape, dt):
            uid[0] += 1
            cm = nc.sbuf_tensor("%s_u%d" % (name, uid[0]), list(shape), dt)
            t = cm.__enter__()
            self.cms.append(cm)
            return t

        def ps(self, name, shape, dt):
            cm = nc.psum_tensor(name, list(shape), dt)
            t = cm.__enter__()
            self.cms.append(cm)
            return t

        def close(self):
            for cm in reversed(self.cms):
                cm.__exit__(None, None, None)
            self.cms = []

    x_in = dram("x", [SEQ, D], F32)
    ctx_in = dram("ctx", [NCTX, D], F32)
    cvec = dram("cvec", [128, 16, 2], F32)
    ada_w = dram("ada_w", [2, D, 6 * D], F32)
    ada_b = dram("ada_b", [128, 2, 96], F32)
    vecs = dram("vecs", [128, 5, 16], F32)
    pvec = dram("pvec", [128, 16], F32)
    mlag = dram("mlag", [128, 6], F32)
    convw = dram("convw", [128, 2, 4, 88], F32)
    w_up = dram("w_up", [2, D, 2 * FF], F32)
    w_down = dram("w_down", [2, FF, D], F32)
    w_in = dram("w_in", [D, 4928], F32)
    w_uq = dram("w_uq", [512, 1536], F32)
    w_ukv = dram("w_ukv", [256, 2048], F32)
    w_out = dram("w_out", [D, D], F32)
    pool_w = dram("pool_w", [4, 512, 512], F32)
    dec = dram("dec", [128, 2, 8], F32)
    epos = dram("epos", [128, 4], F32)
    masks = dram("masks", [128, 2, 128], F32)
    ropeR = dram("ropeR", [128, 16, 2, 64], F32)
    ropeM = dram("ropeM", [128, 16, 2, 32], F32)
    pconst = dram("pconst", [128, 4, 16], F32)
    flipd = dram("flipd", [128, 1], F32)
    out_d = dram("out", [TO, D], F32, kind="ExternalOutput")

    xT_d = scratch("xT_d", [16, 128, TH], F32)
    h0T_d = scratch("h0T_d", [16, 128, NT], BF16)
    qT_d = [scratch("q%dT_d" % s, [8, 128, TH], BF16) for s in (1, 2)]
    kT_d = [scratch("k%dT_d" % s, [8, 128, 1152], BF16) for s in (1, 2)]
    k_d = [scratch("k%d_d" % s, [18, 128, 1024], BF16) for s in (1, 2)]
    vr_d = scratch("vr_d", [18, 128, 1024], BF16)
    cqnT_d = scratch("cqnT_d", [4, 128, TH], BF16)
    ckvnT_d = scratch("ckvnT_d", [2, 128, NT], BF16)
    krT_d = scratch("krT_d", [64, NT], BF16)
    sgT_d = scratch("sgT_d", [8, 128, TH], F32)
    mixT_d = scratch("mixT_d", [16, 128, TH], BF16)

    G = Scope()
    ident_f = G.sb("ident_f", [128, 128], F32)
    ident_b = G.sb("ident_b", [128, 128], BF16)
    ones_b = G.sb("ones_b", [128, 128], BF16)
    one_col = G.sb("one_col", [128, 1], F32)
    scv = G.sb("scv", [128, 16, 2], BF16)
    modT = G.sb("modT", [128, 2, 96], F32)
    cmodT = G.sb("cmodT", [128, 32], F32)
    a1 = G.sb("a1", [128, 2, 16], F32)
    a2 = G.sb("a2", [128, 2, 16], F32)
    ca1 = G.sb("ca1", [128, 16], F32)
    gps = G.sb("gps", [128, 16], F32)
    vecs_s = G.sb("vecs_s", [128, 5, 16], F32)
    pvec_s = G.sb("pvec_s", [128, 16], F32)
    mlag_s = G.sb("mlag_s", [128, 6], F32)
    convw_s = G.sb("convw_s", [128, 2, 4, 88], F32)
    adab_s = G.sb("adab_s", [128, 2, 96], F32)
    flip_s = G.sb("flip_s", [128, 1], F32)
    pconst_s = G.sb("pconst_s", [128, 4, 16], F32)
    PF = [G.ps("pf%d" % i, [128, 512], F32) for i in range(6)]
    PB = [G.ps("pb%d" % i, [128, 1024], BF16) for i in range(2)]
    pf = [Buf(t) for t in PF]
    pb = [Buf(t) for t in PB]
    gconst = Buf(ident_f)

    sem_c = S.dsem("const")

    LA = Scope()
    adw = [Buf(LA.sb("adaw%d" % i, [128, 16, 256], BF16), sem=S.dsem("adaw%d" % i)) for i in range(2)]
    L = Scope()
    cv = L.sb("cv", [128, 16, 2], F32)
    identtmp = L.sb("identtmp", [128, 128], F32)
    for dst, src in [(vecs_s, vecs), (pvec_s, pvec), (mlag_s, mlag), (convw_s, convw), (adab_s, ada_b),
                     (flip_s, flipd), (pconst_s, pconst), (cv, cvec)]:
        S.dma("sp", dst[:], src, sem_c, writes=[gconst])
    S.run("pool", lambda e: e.memset(ident_f[:], 0.0), writes=[gconst])
    S.run("pool", lambda e: e.affine_select(out=ident_f[:], in_=ident_f[:], pattern=[[-1, 128]],
                                            compare_op=ALU.not_equal, fill=1.0, base=0, channel_multiplier=1),
          writes=[gconst])
    S.run("dve", lambda e: e.tensor_copy(out=ident_b[:], in_=ident_f[:]), writes=[gconst])
    S.run("dve", lambda e: e.memset(ones_b[:], 1.0), writes=[gconst])
    S.run("dve", lambda e: e.memset(one_col[:], 1.0), writes=[gconst])
    S.run("act", lambda e: e.activation(out=scv[:], in_=cv[:], func=AF.Silu), writes=[gconst])
    pfa = pf[5]
    modB = Buf(modT)
    wv_l = [ada_w[l].rearrange("(c p) n -> p c n", p=128) for l in range(2)]
    ada_state = {"it": 0}
    pav = PF[5][:, 0:4].rearrange("p (f t) -> p f t", t=2)

    def ada_group(l, g2):
        wb = adw[ada_state["it"] % 2]
        ada_state["it"] += 1
        ncol = 2 if (l == 0 and g2 < 16) else 1
        S.dma("pool", wb.ap[:], wv_l[l][:, :, g2 * 256:(g2 + 1) * 256], wb.sem, writes=[wb])
        fns = []
        for j in range(2):
            for c in range(16):
                fns.append(lambda e, wb=wb, j=j, c=c, ncol=ncol: e.matmul(
                    PF[5][:, j * 2:j * 2 + ncol], lhsT=wb.ap[:, c, j * 128:(j + 1) * 128], rhs=scv[:, c, 0:ncol],
                    start=(c == 0), stop=(c == 15)))
        S.mm(fns, reads=[wb, gconst], writes=[pfa])
        fc0 = 2 * g2
        S.run("dve", lambda e, l=l, fc0=fc0: e.tensor_tensor(out=modT[:, l, fc0:fc0 + 2], in0=pav[:, :, 0],
                                                            in1=adab_s[:, l, fc0:fc0 + 2], op=ALU.add), reads=[pfa, gconst], writes=[modB])
        if ncol == 2:
            S.run("dve", lambda e, fc0=fc0: e.tensor_tensor(out=cmodT[:, fc0:fc0 + 2], in0=pav[:, :, 1],
                                                          in1=adab_s[:, 0, fc0:fc0 + 2], op=ALU.add), reads=[pfa, gconst], writes=[modB])

    for g2 in range(16):
        ada_group(0, g2)
    pending = [(0, g2) for g2 in range(16, 48)] + [(1, g2) for g2 in range(48)]

    def bg(n=1):
        for _ in range(n):
            if pending:
                ada_group(*pending.pop(0))

    S.run("dve", lambda e: e.scalar_tensor_tensor(out=a1[:, 0, :], in0=modT[:, 0, 16:32], scalar=1.0,
                                                  in1=vecs_s[:, 0, :], op0=ALU.add, op1=ALU.mult), reads=[modB], writes=[gconst])
    S.run("dve", lambda e: e.scalar_tensor_tensor(out=ca1[:], in0=cmodT[:, 16:32], scalar=1.0,
                                                  in1=vecs_s[:, 0, :], op0=ALU.add, op1=ALU.mult), reads=[modB], writes=[gconst])
    S.flush()
    L.close()

    def derived_consts():
        S.run("dve", lambda e: e.scalar_tensor_tensor(out=a1[:, 1, :], in0=modT[:, 1, 16:32], scalar=1.0,
                                                      in1=vecs_s[:, 1, :], op0=ALU.add, op1=ALU.mult), reads=[modB], writes=[gconst])
        for l in range(2):
            S.run("dve", lambda e, l=l: e.scalar_tensor_tensor(out=a2[:, l, :], in0=modT[:, l, 64:80], scalar=1.0,
                                                                in1=vecs_s[:, 2 + l, :], op0=ALU.add, op1=ALU.mult), reads=[modB], writes=[gconst])
        S.run("dve", lambda e: e.tensor_tensor(out=gps[:], in0=modT[:, 1, 32:48], in1=pvec_s[:], op=ALU.mult), reads=[modB], writes=[gconst])

    def rstd_from_ss(ss_ap, out_ap, n, reads, writes, eng_tmp):
        S.run("dve", lambda e: e.tensor_scalar(out=eng_tmp, in0=ss_ap, scalar1=1.0 / n, scalar2=EPS,
                                               op0=ALU.mult, op1=ALU.add), reads=reads, writes=writes)
        S.run("act", lambda e: e.activation(out=eng_tmp, in_=eng_tmp, func=AF.Sqrt), writes=writes)
        S.run("dve", lambda e: e.reciprocal(out=out_ap, in_=eng_tmp), writes=writes)

    L12 = Scope()
    h0T = L12.sb("h0T", [128, 16, NT], BF16)
    h0Tb = [Buf(h0T[:, :, i * 128:(i + 1) * 128]) for i in range(18)]
    L = Scope()
    xt = [Buf(L.sb("xt%d" % i, [128, D], F32), sem=S.dsem("xt%d" % i)) for i in range(2)]
    xn = [Buf(L.sb("xn%d" % i, [128, D], BF16)) for i in range(2)]
    junk = Buf(L.sb("junk", [128, D], BF16))
    st = [Buf(L.sb("st%d" % i, [128, 4], F32)) for i in range(2)]
    xst = [Buf(L.sb("xst%d" % i, [128, 16, 128], F32), sem=S.dsem("xst%d" % i)) for i in range(2)]
    for i in range(18):
        xb, xnb, sb_, xs_ = xt[i % 2], xn[i % 2], st[i % 2], xst[i % 2]
        src = x_in[i * 128:(i + 1) * 128, :] if i < 16 else ctx_in[(i - 16) * 128:(i - 15) * 128, :]
        S.dma("sp", xb.ap[:], src, xb.sem, writes=[xb])
        S.run("act", lambda e, xb=xb, sb_=sb_: e.activation(out=junk.ap[:], in_=xb.ap[:], func=AF.Square,
                                                            accum_out=sb_.ap[:, 0:1]), reads=[xb], writes=[junk, sb_])
        rstd_from_ss(sb_.ap[:, 0:1], sb_.ap[:, 2:3], D, [sb_], [sb_], sb_.ap[:, 1:2])
        S.run("act", lambda e, xb=xb, xnb=xnb, sb_=sb_: e.activation(out=xnb.ap[:], in_=xb.ap[:], func=AF.Copy,
                                                                     scale=sb_.ap[:, 2:3]), reads=[xb, sb_], writes=[xnb])
        is_ctx = i >= 16
        bg()
        for c in range(16):
            if i <= 8:
                p = pf[c % 2]
                S.mm([lambda e, xb=xb, c=c, p=p: e.transpose(out=p.ap[:, 0:128], in_=xb.ap[:, c * 128:(c + 1) * 128],
                                                             identity=ident_f[:])], reads=[xb, gconst], writes=[p])
                S.run("act", lambda e, c=c, p=p, xs_=xs_: e.activation(
                    out=xs_.ap[:, c, :], in_=p.ap[:, 0:128], func=AF.Copy), reads=[p], writes=[xs_])
            p = pb[c % 2]
            S.mm([lambda e, xnb=xnb, c=c, p=p: e.transpose(out=p.ap[:, 0:128], in_=xnb.ap[:, c * 128:(c + 1) * 128],
                                                           identity=ident_b[:])], reads=[xnb, gconst], writes=[p])
            sc_ap = ca1[:, c:c + 1] if is_ctx else a1[:, 0, c:c + 1]
            sh_ap = cmodT[:, c:c + 1] if is_ctx else modT[:, 0, c:c + 1]
            S.run("dve", lambda e, i=i, c=c, p=p, sc_ap=sc_ap, sh_ap=sh_ap: e.tensor_scalar(
                out=h0T[:, c, i * 128:(i + 1) * 128], in0=p.ap[:, 0:128], scalar1=sc_ap, scalar2=sh_ap,
                op0=ALU.mult, op1=ALU.add), reads=[p, gconst], writes=[h0Tb[i]])
        if i <= 8:
            ncols = 128 if i < 8 else 16
            S.dma("sp", xT_d[:, :, i * 128:i * 128 + ncols].rearrange("c p t -> p c t"), xs_.ap[:, :, 0:ncols],
                  xs_.sem, reads=[xs_])
    if debug == "p1":
        sem = S.dsem("dump")
        S.dma("sp", h0T_d.rearrange("c p t -> p c t"), h0T[:], sem, reads=h0Tb)
    S.flush()
    L.close()
    if debug == "p1":
        L12.close()
        return finish(nc, S, G)

    L = Scope()
    wg = [Buf(L.sb("wg%d" % i, [128, 16, 512], BF16), sem=S.dsem("wg%d" % i)) for i in range(2)]
    ropeR_s = L.sb("ropeR_s", [128, 16, 2, 64], F32)
    ropeM_s = L.sb("ropeM_s", [128, 16, 2, 32], F32)
    dec_s = L.sb("dec_s", [128, 2, 8], F32)
    epos_s = L.sb("epos_s", [128, 4], F32)
    qd = L.sb("qd", [128, 2, 8], F32)
    kd = L.sb("kd", [128, 2, 8], F32)
    tabs = Buf(ropeR_s)
    for dst, src in [(ropeR_s, ropeR), (ropeM_s, ropeM), (dec_s, dec), (epos_s, epos)]:
        S.dma("sp", dst[:], src, sem_c, writes=[tabs])
    SKIP = os.environ.get("P2SKIP", "")
    if "t" not in SKIP:
        S.run("act", lambda e: e.activation(out=dec_s[:], in_=dec_s[:], func=AF.Exp, scale=-1.0), writes=[tabs])
        if "l" not in SKIP:
            S.run("act", lambda e: e.activation(out=dec_s[:], in_=dec_s[:], func=AF.Ln, bias=one_col[:]), reads=[gconst], writes=[tabs])
    for s_ in range(2 if "t" not in SKIP else 0):
        S.run("act", lambda e, s_=s_: e.activation(out=qd[:, s_, :], in_=dec_s[:, s_, :], func=AF.Exp,
                                                   scale=epos_s[:, 2 * s_ + 1:2 * s_ + 2]), writes=[tabs])
        S.run("act", lambda e, s_=s_: e.activation(out=kd[:, s_, :], in_=dec_s[:, s_, :], func=AF.Exp,
                                                   scale=epos_s[:, 2 * s_:2 * s_ + 1]), writes=[tabs])
    S.run("dve", lambda e: e.tensor_scalar(out=kd[:], in0=kd[:], scalar1=float(128 ** -0.5), scalar2=None,
                                           op0=ALU.mult), writes=[tabs])

    w_in_v = w_in.rearrange("(c p) n -> p c n", p=128)
    NR = 3
    rr = [Buf(L.sb("rr%d" % i, [128, 512], F32)) for i in range(NR)]
    tA = [Buf(L.sb("tA%d" % i, [128, 256], F32)) for i in range(NR)]
    tB = [Buf(L.sb("tB%d" % i, [128, 256], F32)) for i in range(NR)]
    ob = [[Buf(L.sb("ob%d_%d" % (s_, i), [128, 512], BF16), sem=S.dsem("ob%d_%d" % (s_, i))) for i in range(NR)]
          for s_ in range(2)]
    tst = [[Buf(L.sb("tst%d_%d" % (s_, i), [128, 4, 128], BF16), sem=S.dsem("tst%d_%d" % (s_, i))) for i in range(NR)]
           for s_ in range(2)]
    sst = [Buf(L.sb("sst%d" % i, [128, 4], F32)) for i in range(NR)]
    sgst = [Buf(L.sb("sgst%d" % i, [128, TH], F32), sem=S.dsem("sgst%d" % i)) for i in range(2)]
    cnt = {"z": 0, "w": 0, "t": 0, "pb": 0}

    def load_wgroup(c0, ncols):
        wb = wg[cnt["w"] % 2]
        cnt["w"] += 1
        if "w" in SKIP:
            return wb
        S.dma("pool", wb.ap[:, :, 0:ncols], w_in_v[:, :, c0:c0 + ncols], wb.sem, writes=[wb])
        return wb

    def proj_tile(wb, ncols, i):
        p = pf[cnt["z"] % 3]
        cnt["z"] += 1
        bg()
        S.mm([lambda e, c=c, p=p, wb=wb: e.matmul(p.ap[:, 0:ncols], lhsT=h0T[:, c, i * 128:(i + 1) * 128],
                                                   rhs=wb.ap[:, c, 0:ncols], start=(c == 0), stop=(c == 15))
              for c in range(16)], reads=[wb, h0Tb[i]], writes=[p])
        return p

    def rope_tm(p, nh, hd, cos_ap, sin_ap, dst, rd=(), tA_=None, tB_=None):
        h2 = hd // 2
        zv = p.ap[:, 0:nh * hd].rearrange("p (h two j) -> p h two j", h=nh, two=2)
        x1, x2 = zv[:, :, 0, :], zv[:, :, 1, :]
        cb = cos_ap.unsqueeze(1).broadcast_to([128, nh, h2])
        sb2 = sin_ap.unsqueeze(1).broadcast_to([128, nh, h2])
        dv = dst.ap[:, 0:nh * hd].rearrange("p (h two j) -> p h two j", h=nh, two=2)
        ta = tA_.ap[:, 0:nh * h2].rearrange("p (h j) -> p h j", h=nh)
        tb = tB_.ap[:, 0:nh * h2].rearrange("p (h j) -> p h j", h=nh)
        S.run("dve", lambda e: e.tensor_tensor(out=ta, in0=x1, in1=cb, op=ALU.mult), reads=[p, tabs], writes=[tA_])
        S.run("dve", lambda e: e.tensor_tensor(out=tb, in0=x2, in1=sb2, op=ALU.mult), reads=[p, tabs], writes=[tB_])
        S.run("dve", lambda e: e.tensor_tensor(out=dv[:, :, 0, :], in0=ta, in1=tb, op=ALU.subtract),
              reads=[tA_, tB_], writes=[dst])
        S.run("dve", lambda e: e.tensor_tensor(out=ta, in0=x1, in1=sb2, op=ALU.mult), reads=[p, tabs], writes=[tA_])
        S.run("dve", lambda e: e.tensor_tensor(out=tb, in0=x2, in1=cb, op=ALU.mult), reads=[p, tabs], writes=[tB_])
        S.run("dve", lambda e: e.tensor_tensor(out=dv[:, :, 1, :], in0=ta, in1=tb, op=ALU.add),
              reads=[tA_, tB_], writes=[dst])

    def transposes_out(src_buf, nblk, blk, dst_dram_ap_fn, npart_out, i, stg, gain_ap_fn=None):
        ncols = 128 if i != 8 else 16
        for j in range(nblk):
            p = pb[cnt["pb"] % 2]
            cnt["pb"] += 1
            S.mm([lambda e, j=j, p=p: e.transpose(out=p.ap[0:blk, 0:128], in_=src_buf.ap[:, j * blk:(j + 1) * blk],
                                                  identity=ident_b[:])], reads=[src_buf, gconst], writes=[p])
            if gain_ap_fn is None:
                S.run("act", lambda e, j=j, p=p: e.activation(out=stg.ap[0:blk, j, :], in_=p.ap[0:blk, 0:128], func=AF.Copy),
                      reads=[p], writes=[stg])
            else:
                S.run("act", lambda e, j=j, p=p: e.activation(out=stg.ap[0:blk, j, :], in_=p.ap[0:blk, 0:128], func=AF.Copy,
                                                              scale=gain_ap_fn(j)), reads=[p, gconst], writes=[stg])
        S.dma("sp", dst_dram_ap_fn(ncols), stg.ap[0:blk, 0:nblk, 0:ncols], stg.sem, reads=[stg])

    LIM = int(os.environ.get("P2LIM", "99"))
    for kind in (("q", "k") if LIM >= 2 else ("q",) if LIM >= 1 else ()):
        for gi in range(2):
            wb = load_wgroup((0 if kind == "q" else 1024) + gi * 512, 512)
            tiles = range(9) if kind == "q" else range(18)
            for i in tiles:
                r = cnt["t"] % NR
                cnt["t"] += 1
                p = proj_tile(wb, 512, i)
                rrb = rr[r]
                if i < 16:
                    rope_tm(p, 4, 128, ropeR_s[:, i, 0, :], ropeR_s[:, i, 1, :], rrb, tA_=tA[r], tB_=tB[r])
                else:
                    S.run("act", lambda e, p=p, rrb=rrb: e.activation(out=rrb.ap[:], in_=p.ap[:], func=AF.Copy),
                          reads=[p], writes=[rrb])
                sc_t = qd if kind == "q" else kd
                for s_ in range(2):
                    o = ob[s_][r]
                    for hh in range(4):
                        S.run("dve" if hh % 2 == 0 else "act", (lambda e, o=o, rrb=rrb, s_=s_, sc_t=sc_t, gi=gi, hh=hh: e.tensor_scalar(
                            out=o.ap[:, hh * 128:(hh + 1) * 128], in0=rrb.ap[:, hh * 128:(hh + 1) * 128],
                            scalar1=sc_t[:, s_, gi * 4 + hh:gi * 4 + hh + 1], scalar2=None, op0=ALU.mult)) if hh % 2 == 0 else
                            (lambda e, o=o, rrb=rrb, s_=s_, sc_t=sc_t, gi=gi, hh=hh: e.activation(
                            out=o.ap[:, hh * 128:(hh + 1) * 128], in_=rrb.ap[:, hh * 128:(hh + 1) * 128], func=AF.Copy,
                            scale=sc_t[:, s_, gi * 4 + hh:gi * 4 + hh + 1])),
                            reads=[rrb, tabs], writes=[o])
                    if kind == "k":
                        S.dma("sp", k_d[s_][i][:, gi * 512:(gi + 1) * 512], o.ap[:], o.sem, reads=[o])
                    if i <= 8:
                        dd = (qT_d if kind == "q" else kT_d)[s_]
                        transposes_out(o, 4, 128, lambda ncols, dd=dd, gi=gi, i=i: dd[gi * 4:gi * 4 + 4, :, i * 128:i * 128 + ncols]
                                       .rearrange("h p t -> p h t"), 128, i if kind == "q" else -1, tst[s_][r])
    for gi in range(2 if LIM >= 3 else 0):
        wb = load_wgroup(2048 + gi * 512, 512)
        for i in range(18):
            r = cnt["t"] % NR
            cnt["t"] += 1
            p = proj_tile(wb, 512, i)
            o = ob[0][r]
            S.run("act", lambda e, p=p, o=o: e.activation(out=o.ap[:], in_=p.ap[:], func=AF.Copy), reads=[p], writes=[o])
            S.dma("sp", vr_d[i][:, gi * 512:(gi + 1) * 512], o.ap[:], o.sem, reads=[o])
    wb = load_wgroup(4096, 512)
    for i in range(9 if LIM >= 4 else 0):
        r = cnt["t"] % NR
        cnt["t"] += 1
        p = proj_tile(wb, 512, i)
        sb_ = sst[r]
        rrb = rr[r]
        S.run("act", lambda e, p=p, sb_=sb_, rrb=rrb: e.activation(out=rrb.ap[:], in_=p.ap[:], func=AF.Square,
                                                                   accum_out=sb_.ap[:, 0:1]), reads=[p], writes=[rrb, sb_])
        rstd_from_ss(sb_.ap[:, 0:1], sb_.ap[:, 2:3], 512, [sb_], [sb_], sb_.ap[:, 1:2])
        o = ob[0][r]
        S.run("act", lambda e, p=p, o=o, sb_=sb_: e.activation(out=o.ap[:], in_=p.ap[:], func=AF.Copy, scale=sb_.ap[:, 2:3]),
              reads=[p, sb_], writes=[o])
        transposes_out(o, 4, 128, lambda ncols, i=i: cqnT_d[:, :, i * 128:i * 128 + ncols].rearrange("c p t -> p c t"),
                       128, i, tst[0][r], gain_ap_fn=lambda j: mlag_s[:, j:j + 1])
    wb = load_wgroup(4608, 320)
    for i in range(18 if LIM >= 5 else 0):
        r = cnt["t"] % NR
        cnt["t"] += 1
        p = proj_tile(wb, 320, i)
        sb_ = sst[r]
        rrb = rr[r]
        S.run("act", lambda e, p=p, sb_=sb_, rrb=rrb: e.activation(out=rrb.ap[:, 0:256], in_=p.ap[:, 0:256], func=AF.Square,
                                                                   accum_out=sb_.ap[:, 0:1]), reads=[p], writes=[rrb, sb_])
        rstd_from_ss(sb_.ap[:, 0:1], sb_.ap[:, 2:3], 256, [sb_], [sb_], sb_.ap[:, 1:2])
        o = ob[0][r]
        S.run("act", lambda e, p=p, o=o, sb_=sb_: e.activation(out=o.ap[:, 0:256], in_=p.ap[:, 0:256], func=AF.Copy,
                                                               scale=sb_.ap[:, 2:3]), reads=[p, sb_], writes=[o])
        transposes_out(o, 2, 128, lambda ncols, i=i: ckvnT_d[:, :, i * 128:i * 128 + ncols].rearrange("c p t -> p c t"),
                       128, -1, tst[0][r], gain_ap_fn=lambda j: mlag_s[:, 4 + j:5 + j])
        o2 = ob[1][r]
        if i < 16:
            pk = Buf(p.ap[:, 256:320])
            pk.w, pk.r = p.w, p.r
            rope_tm(pk, 1, 64, ropeM_s[:, i, 0, :], ropeM_s[:, i, 1, :], rrb, tA_=tA[r], tB_=tB[r])
            p.r.update(pk.r)
            S.run("act", lambda e, o2=o2, rrb=rrb: e.activation(out=o2.ap[:, 0:64], in_=rrb.ap[:, 0:64], func=AF.Copy),
                  reads=[rrb], writes=[o2])
        else:
            S.run("act", lambda e, o2=o2, p=p: e.activation(out=o2.ap[:, 0:64], in_=p.ap[:, 256:320], func=AF.Copy),
                  reads=[p], writes=[o2])
        transposes_out(o2, 1, 64, lambda ncols, i=i: krT_d[:, i * 128:i * 128 + ncols].unsqueeze(1),
                       64, -1, tst[1][r])
    for gi in range(2 if LIM >= 6 else 0):
        wb = load_wgroup(3072 + gi * 512, 512)
        for hh in range(4):
            h = gi * 4 + hh
            sg_ = sgst[h % 2]
            for (t0, t1) in TG:
                p = pf[3 + cnt["z"] % 2]
                cnt["z"] += 1
                S.mm([lambda e, c=c, p=p, wb=wb, hh=hh, t0=t0, t1=t1: e.matmul(
                    p.ap[:, 0:t1 - t0], lhsT=wb.ap[:, c, hh * 128:(hh + 1) * 128], rhs=h0T[:, c, t0:t1],
                    start=(c == 0), stop=(c == 15)) for c in range(16)], reads=[wb] + h0Tb[0:9], writes=[p])
                S.run("act", lambda e, p=p, sg_=sg_, t0=t0, t1=t1: e.activation(out=sg_.ap[:, t0:t1], in_=p.ap[:, 0:t1 - t0],
                                                                             func=AF.Silu), reads=[p], writes=[sg_])
            S.dma("sp", sgT_d[h], sg_.ap[:], sg_.sem, reads=[sg_])
    S.flush()
    L.close()
    L12.close()
    if debug == "p2":
        return finish(nc, S, G)

    rot = {"n": 0}

    def nxt(lst):
        key = tuple(id(b) for b in lst)
        rot[key] = rot.get(key, -1) + 1
        return lst[rot[key] % len(lst)]

    L = Scope()
    dec_s = L.sb("dec3", [128, 2, 8], F32)
    cdt = L.sb("cdt", [128, 2, 8], F32)
    mask_s = L.sb("mask_s", [128, 2, 128], F32)
    Sst = L.sb("Sst", [128, 8, 128], F32)
    Sbf = L.sb("Sbf", [128, 8, 128], BF16)
    Tst = L.sb("Tst", [128, 8, 128], F32)
    oT = L.sb("oT", [128, 8, TH], F32)
    t3 = Buf(dec_s)
    Sb = [Buf(Sst[:, h, :]) for h in range(8)]
    Sbb = [Buf(Sbf[:, h, :]) for h in range(8)]
    Tb = [Buf(Tst[:, 4 * g:4 * g + 4, :]) for g in range(2)]
    oTb = [Buf(oT[:, h, :]) for h in range(8)]
    S.dma("sp", dec_s[:], dec, sem_c, writes=[t3])
    S.dma("sp", mask_s[:], masks, sem_c, writes=[t3])
    S.run("act", lambda e: e.activation(out=dec_s[:], in_=dec_s[:], func=AF.Exp, scale=-1.0), writes=[t3])
    S.run("act", lambda e: e.activation(out=dec_s[:], in_=dec_s[:], func=AF.Ln, bias=one_col[:]), reads=[gconst], writes=[t3])
    S.run("act", lambda e: e.activation(out=cdt[:], in_=dec_s[:], func=AF.Exp, scale=-128.0), writes=[t3])
    kt_ = [Buf(L.sb("kt%d" % i, [128, 1024], BF16), sem=S.dsem("kt%d" % i)) for i in range(2)]
    vt_ = [Buf(L.sb("vt%d" % i, [128, 1024], BF16), sem=S.dsem("vt%d" % i)) for i in range(2)]
    qc_ = [Buf(L.sb("qc%d" % i, [128, 8, 128], BF16), sem=S.dsem("qc%d" % i)) for i in range(2)]
    kc_ = [Buf(L.sb("kc%d" % i, [128, 8, 128], BF16), sem=S.dsem("kc%d" % i)) for i in range(2)]
    At_ = [Buf(L.sb("At%d" % i, [128, 128], BF16)) for i in range(3)]
    ci = 0
    for s_ in range(2):
        order = [16, 17] + list(range(9)) if s_ == 0 else [17, 16] + list(range(15, -1, -1))
        S.run("dve", lambda e: e.memset(Sst[:], 0.0), writes=Sb)
        S.run("dve", lambda e: e.memset(Sbf[:], 0.0), writes=Sbb)
        for oi, c in enumerate(order):
            kt, vt = kt_[ci % 2], vt_[ci % 2]
            qc, kc = qc_[ci % 2], kc_[ci % 2]
            ci += 1
            S.dma("sp", kt.ap[:], k_d[s_][c], kt.sem, writes=[kt])
            S.dma("sp", vt.ap[:], vr_d[c], vt.sem, writes=[vt])
            if c <= 8:
                nq = 128 if c < 8 else 16
                S.dma("sp", qc.ap[:, :, 0:nq], qT_d[s_][:, :, c * 128:c * 128 + nq].rearrange("h p t -> p h t"), qc.sem,
                      writes=[qc])
                S.dma("sp", kc.ap[:], kT_d[s_][:, :, c * 128:(c + 1) * 128].rearrange("h p t -> p h t"), kc.sem, writes=[kc])
                for h in range(8):
                    pi = nxt(pf[0:3])
                    S.mm([lambda e, h=h, pi=pi, kc=kc, qc=qc, nq=nq: e.matmul(pi.ap[:, 0:nq], lhsT=kc.ap[:, h, :], rhs=qc.ap[:, h, 0:nq],
                                                                           start=True, stop=True)], reads=[kc, qc], writes=[pi])
                    At = nxt(At_)
                    S.run("dve", lambda e, pi=pi, At=At, nq=nq, s_=s_: e.tensor_tensor(out=At.ap[:, 0:nq], in0=pi.ap[:, 0:nq],
                                                                                 in1=mask_s[:, s_, 0:nq], op=ALU.mult),
                          reads=[pi, t3], writes=[At])
                    po = nxt(pf[3:5])
                    S.mm([lambda e, h=h, po=po, vt=vt, At=At, nq=nq: e.matmul(po.ap[:, 0:nq], lhsT=vt.ap[:, h * 128:(h + 1) * 128],
                                                                           rhs=At.ap[:, 0:nq], start=True, stop=False),
                          lambda e, h=h, po=po, qc=qc, nq=nq: e.matmul(po.ap[:, 0:nq], lhsT=Sbf[:, h, :], rhs=qc.ap[:, h, 0:nq],
                                                                     start=False, stop=True)],
                         reads=[vt, At, Sbb[h], qc], writes=[po])
                    if s_ == 0:
                        S.run("act", lambda e, h=h, po=po, c=c, nq=nq: e.activation(out=oT[:, h, c * 128:c * 128 + nq], in_=po.ap[:, 0:nq],
                                                                                   func=AF.Copy), reads=[po], writes=[oTb[h]])
                    else:
                        S.run("dve", lambda e, h=h, po=po, c=c, nq=nq: e.tensor_tensor(out=oT[:, h, c * 128:c * 128 + nq], in0=po.ap[:, 0:nq],
                                                                                      in1=oT[:, h, c * 128:c * 128 + nq], op=ALU.add),
                              reads=[po], writes=[oTb[h]])
            if oi == len(order) - 1:
                continue
            for g in range(2):
                pp = nxt(pf[0:3])
                S.mm([lambda e, g=g, hh=hh, pp=pp, kt=kt, vt=vt: e.matmul(
                    pp.ap[:, hh * 128:(hh + 1) * 128], lhsT=kt.ap[:, (4 * g + hh) * 128:(4 * g + hh + 1) * 128],
                    rhs=vt.ap[:, (4 * g + hh) * 128:(4 * g + hh + 1) * 128], start=True, stop=True) for hh in range(4)],
                    reads=[kt, vt], writes=[pp])
                S.run("dve", lambda e, g=g, pp=pp: e.tensor_tensor(out=Tst[:, 4 * g:4 * g + 4, :].rearrange("p h d -> p (h d)"),
                                                                  in0=pp.ap[:, :], in1=Sst[:, 4 * g:4 * g + 4, :].rearrange("p h d -> p (h d)"),
                                                                  op=ALU.add), reads=[pp] + Sb[4 * g:4 * g + 4], writes=[Tb[g]])
                for hh in range(4):
                    h = 4 * g + hh
                    S.run("act", lambda e, h=h, s_=s_: e.activation(out=Sst[:, h, :], in_=Tst[:, h, :], func=AF.Copy,
                                                                   scale=cdt[:, s_, h:h + 1]), reads=[Tb[g], t3], writes=[Sb[h]])
                    S.run("act", lambda e, h=h, s_=s_: e.activation(out=Sbf[:, h, :], in_=Tst[:, h, :], func=AF.Copy,
                                                                   scale=cdt[:, s_, h:h + 1]), reads=[Tb[g], t3], writes=[Sbb[h]])
    if debug == "p3":
        S.dma("sp", sgT_d.rearrange("h p t -> p h t"), oT[:], sem_c, reads=oTb)
    sq_ = [Buf(L.sb("sq%d" % i, [128, TH], BF16)) for i in range(2)]
    rs_ = [Buf(L.sb("rs%d" % i, [128, TH], F32)) for i in range(2)]
    sg_ = [Buf(L.sb("sg%d" % i, [128, TH], F32), sem=S.dsem("sg3_%d" % i)) for i in range(2)]
    mo_ = [Buf(L.sb("mo%d" % i, [128, TH], BF16), sem=S.dsem("mo%d" % i)) for i in range(2)]
    if debug != "p3":
        for h in range(8):
            sq, rs, sg, mo = sq_[h % 2], rs_[h % 2], sg_[h % 2], mo_[h % 2]
            S.dma("sp", sg.ap[:], sgT_d[h], sg.sem, writes=[sg])
            S.run("act", lambda e, h=h, sq=sq: e.activation(out=sq.ap[:], in_=oT[:, h, :], func=AF.Square), reads=[oTb[h]], writes=[sq])
            for (t0, t1) in TG:
                pi = nxt(pf[0:5])
                S.mm([lambda e, pi=pi, sq=sq, t0=t0, t1=t1: e.matmul(pi.ap[:, 0:t1 - t0], lhsT=ones_b[:], rhs=sq.ap[:, t0:t1],
                                                                      start=True, stop=True)], reads=[sq, gconst], writes=[pi])
                S.run("dve", lambda e, pi=pi, rs=rs, t0=t0, t1=t1: e.tensor_scalar(out=rs.ap[:, t0:t1], in0=pi.ap[:, 0:t1 - t0],
                                                                                  scalar1=1.0 / 128, scalar2=EPS, op0=ALU.mult, op1=ALU.add),
                      reads=[pi], writes=[rs])
            S.run("act", lambda e, rs=rs: e.activation(out=rs.ap[:], in_=rs.ap[:], func=AF.Sqrt), writes=[rs])
            S.run("dve", lambda e, rs=rs: e.reciprocal(out=rs.ap[:], in_=rs.ap[:]), writes=[rs])
            S.run("dve", lambda e, rs=rs, h=h: e.tensor_tensor(out=rs.ap[:], in0=rs.ap[:], in1=oT[:, h, :], op=ALU.mult),
                  reads=[oTb[h]], writes=[rs])
            S.run("dve", lambda e, rs=rs, sg=sg, mo=mo: e.tensor_tensor(out=mo.ap[:], in0=rs.ap[:], in1=sg.ap[:], op=ALU.mult),
                  reads=[rs, sg], writes=[mo])
            S.dma("sp", mixT_d[h], mo.ap[:], mo.sem, reads=[mo])
    bg(len(pending))
    S.flush()
    L.close()
    LA.close()
    if debug == "p3":
        return finish(nc, S, G)
    derived_consts()

    L = Scope()
    wuq = Buf(L.sb("wuq", [128, 4, 1536], BF16), sem=S.dsem("wuq"))
    wukv = Buf(L.sb("wukv", [128, 2, 2048], BF16), sem=S.dsem("wukv"))
    cqn = Buf(L.sb("cqn", [128, 4, TH], BF16), sem=S.dsem("cqn"))
    ckvn = Buf(L.sb("ckvn", [128, 2, NT], BF16), sem=S.dsem("ckvn"))
    krT = Buf(L.sb("krT", [128, NT], BF16), sem=S.dsem("krT"))
    ropeM_s = L.sb("ropeM4", [128, 16, 2, 32], F32)
    rM = Buf(ropeM_s, sem=sem_c)
    qnT = L.sb("qnT", [128, 8, TH], BF16)
    qrT = L.sb("qrT", [128, 8, TH], BF16)
    qnTb = [Buf(qnT[:, h, :]) for h in range(8)]
    qrTb = Buf(qrT)
    S.dma("pool", wuq.ap[:], w_uq.rearrange("(c p) n -> p c n", p=128), wuq.sem, writes=[wuq])
    S.dma("pool", wukv.ap[:], w_ukv.rearrange("(c p) n -> p c n", p=128), wukv.sem, writes=[wukv])
    S.dma("sp", cqn.ap[:], cqnT_d.rearrange("c p t -> p c t"), cqn.sem, writes=[cqn])
    S.dma("sp", ckvn.ap[:], ckvnT_d.rearrange("c p t -> p c t"), ckvn.sem, writes=[ckvn])
    S.run("dve", lambda e: e.memset(krT.ap[64:128, :], 0.0), writes=[krT])
    S.run("dve", lambda e: e.memset(qrT[64:128, :, :], 0.0), writes=[qrTb])
    S.dma("sp", krT.ap[0:64, :], krT_d, krT.sem, writes=[krT])
    S.dma("sp", ropeM_s[:], ropeM, sem_c, writes=[rM])
    P4LIM = int(os.environ.get("P4LIM", "99"))
    for h in range(8 if P4LIM >= 1 else 0):
        for (t0, t1) in TG:
            pi = nxt(pf[0:6])
            S.mm([lambda e, c=c, h=h, pi=pi, t0=t0, t1=t1: e.matmul(pi.ap[:, 0:t1 - t0], lhsT=wuq.ap[:, c, h * 192:h * 192 + 128],
                                                                     rhs=cqn.ap[:, c, t0:t1], start=(c == 0), stop=(c == 3))
                  for c in range(4)], reads=[wuq, cqn], writes=[pi])
            S.run("act", lambda e, h=h, pi=pi, t0=t0, t1=t1: e.activation(out=qnT[:, h, t0:t1], in_=pi.ap[:, 0:t1 - t0], func=AF.Copy),
                  reads=[pi], writes=[qnTb[h]])
    rr4 = [Buf(L.sb("rr4_%d" % i, [128, 512], F32)) for i in range(2)]
    ta4 = [Buf(L.sb("ta4_%d" % i, [128, 256], F32)) for i in range(2)]
    tb4 = [Buf(L.sb("tb4_%d" % i, [128, 256], F32)) for i in range(2)]
    qrb = [Buf(L.sb("qrb%d" % i, [128, 512], BF16)) for i in range(2)]
    wq3 = wuq.ap[:].rearrange("p c (h x) -> p c h x", h=8)
    for i in range(9 if P4LIM >= 2 else 0):
        m = 128 if i < 8 else 16
        pi = nxt(pf[0:6])
        S.mm([lambda e, c=c, pi=pi, i=i, m=m: e.matmul(pi.ap[0:m, :], lhsT=cqn.ap[:, c, i * 128:i * 128 + m], rhs=wq3[:, c, :, 128:192],
                                                       start=(c == 0), stop=(c == 3)) for c in range(4)], reads=[wuq, cqn], writes=[pi])
        rrb, ta, tb, qb = rr4[i % 2], ta4[i % 2], tb4[i % 2], qrb[i % 2]
        zv = pi.ap[0:m, :].rearrange("p (h two j) -> p h two j", h=8, two=2)
        x1, x2 = zv[:, :, 0, :], zv[:, :, 1, :]
        cb = ropeM_s[0:m, i, 0, :].unsqueeze(1).broadcast_to([m, 8, 32])
        sb2 = ropeM_s[0:m, i, 1, :].unsqueeze(1).broadcast_to([m, 8, 32])
        dv = rrb.ap[0:m, :].rearrange("p (h two j) -> p h two j", h=8, two=2)
        tav = ta.ap[0:m, :].rearrange("p (h j) -> p h j", h=8)
        tbv = tb.ap[0:m, :].rearrange("p (h j) -> p h j", h=8)
        S.run("dve", lambda e, tav=tav, x1=x1, cb=cb: e.tensor_tensor(out=tav, in0=x1, in1=cb, op=ALU.mult), reads=[pi, rM], writes=[ta])
        S.run("dve", lambda e, tbv=tbv, x2=x2, sb2=sb2: e.tensor_tensor(out=tbv, in0=x2, in1=sb2, op=ALU.mult), reads=[pi, rM], writes=[tb])
        S.run("dve", lambda e, dv=dv, tav=tav, tbv=tbv: e.tensor_tensor(out=dv[:, :, 0, :], in0=tav, in1=tbv, op=ALU.subtract),
              reads=[ta, tb], writes=[rrb])
        S.run("dve", lambda e, tav=tav, x1=x1, sb2=sb2: e.tensor_tensor(out=tav, in0=x1, in1=sb2, op=ALU.mult), reads=[pi, rM], writes=[ta])
        S.run("dve", lambda e, tbv=tbv, x2=x2, cb=cb: e.tensor_tensor(out=tbv, in0=x2, in1=cb, op=ALU.mult), reads=[pi, rM], writes=[tb])
        S.run("dve", lambda e, dv=dv, tav=tav, tbv=tbv: e.tensor_tensor(out=dv[:, :, 1, :], in0=tav, in1=tbv, op=ALU.add),
              reads=[ta, tb], writes=[rrb])
        S.run("act", lambda e, qb=qb, rrb=rrb, m=m: e.activation(out=qb.ap[0:m, :], in_=rrb.ap[0:m, :], func=AF.Copy), reads=[rrb], writes=[qb])
        for h in range(8):
            p = nxt(pb)
            S.mm([lambda e, h=h, p=p, qb=qb, m=m: e.transpose(out=p.ap[0:64, 0:m], in_=qb.ap[0:m, h * 64:(h + 1) * 64],
                                                             identity=ident_b[0:m, 0:m])], reads=[qb, gconst], writes=[p])
            S.run("act", lambda e, h=h, p=p, i=i, m=m: e.activation(out=qrT[0:64, h, i * 128:i * 128 + m], in_=p.ap[0:64, 0:m], func=AF.Copy),
                  reads=[p], writes=[qrTb])
    knT_ = [Buf(L.sb("knT%d" % i, [128, NT], BF16)) for i in range(2)]
    vm_ = [Buf(L.sb("vm%d" % i, [128, 18, 128], BF16)) for i in range(2)]
    PT_ = [Buf(L.sb("PT%d" % i, [128, 512], BF16)) for i in range(3)]
    rd_ = [Buf(L.sb("rd%d" % i, [128, 512], F32)) for i in range(2)]
    os_ = [Buf(L.sb("os%d" % i, [128, TH], BF16), sem=S.dsem("os%d" % i)) for i in range(2)]
    KG = [(0, 512), (512, 1024), (1024, 1536), (1536, 2048), (2048, 2304)]
    scl = float(192 ** -0.5)
    for h in range(8 if P4LIM >= 3 else 0):
        knT, vm, osb = knT_[h % 2], vm_[h % 2], os_[h % 2]
        for (k0, k1) in KG:
            pi = nxt(pf[0:2])
            S.mm([lambda e, c=c, h=h, pi=pi, k0=k0, k1=k1: e.matmul(pi.ap[:, 0:k1 - k0], lhsT=wukv.ap[:, c, h * 256:h * 256 + 128],
                                                                     rhs=ckvn.ap[:, c, k0:k1], start=(c == 0), stop=(c == 1))
                  for c in range(2)], reads=[wukv, ckvn], writes=[pi])
            S.run("act", lambda e, pi=pi, knT=knT, k0=k0, k1=k1: e.activation(out=knT.ap[:, k0:k1], in_=pi.ap[:, 0:k1 - k0], func=AF.Copy),
                  reads=[pi], writes=[knT])
        for k4 in range(5):
            nk = 4 if k4 < 4 else 2
            pi = nxt(pf[0:2])
            fns = []
            for kk in range(nk):
                ktile = k4 * 4 + kk
                for c in range(2):
                    fns.append(lambda e, c=c, h=h, pi=pi, kk=kk, ktile=ktile: e.matmul(
                        pi.ap[:, kk * 128:(kk + 1) * 128], lhsT=ckvn.ap[:, c, ktile * 128:(ktile + 1) * 128],
                        rhs=wukv.ap[:, c, h * 256 + 128:h * 256 + 256], start=(c == 0), stop=(c == 1)))
            S.mm(fns, reads=[wukv, ckvn], writes=[pi])
            S.run("dve", lambda e, pi=pi, vm=vm, k4=k4, nk=nk: e.tensor_copy(
                out=vm.ap[:, k4 * 4:k4 * 4 + nk, :].rearrange("p k d -> p (k d)"), in_=pi.ap[:, 0:nk * 128]), reads=[pi], writes=[vm])
        for (t0, t1) in (TG if P4LIM >= 4 else []):
            n = t1 - t0
            po, pd = nxt(pf[2:4]), nxt(pf[4:6])
            def st_mm(kt, knT=knT, h=h, t0=t0, t1=t1, n=n):
                ps_ = nxt(pf[0:2])
                S.mm([lambda e: e.matmul(ps_.ap[:, 0:n], lhsT=knT.ap[:, kt * 128:(kt + 1) * 128], rhs=qnT[:, h, t0:t1],
                                         start=True, stop=False),
                      lambda e: e.matmul(ps_.ap[:, 0:n], lhsT=krT.ap[:, kt * 128:(kt + 1) * 128], rhs=qrT[:, h, t0:t1],
                                         start=False, stop=True)],
                     reads=[knT, krT, qnTb[h], qrTb], writes=[ps_])
                return ps_
            ps_next = st_mm(0)
            for kt in range(18):
                ps_ = ps_next
                if kt + 1 < 18:
                    ps_next = st_mm(kt + 1)
                PT = nxt(PT_)
                S.run("act", lambda e, ps_=ps_, PT=PT, n=n: e.activation(out=PT.ap[:, 0:n], in_=ps_.ap[:, 0:n], func=AF.Exp, scale=scl),
                      reads=[ps_], writes=[PT])
                S.mm([lambda e, po=po, vm=vm, kt=kt, PT=PT, n=n: e.matmul(po.ap[:, 0:n], lhsT=vm.ap[:, kt, :], rhs=PT.ap[:, 0:n],
                                                                       start=(kt == 0), stop=(kt == 17)),
                      lambda e, pd=pd, kt=kt, PT=PT, n=n: e.matmul(pd.ap[:, 0:n], lhsT=ones_b[:], rhs=PT.ap[:, 0:n],
                                                                 start=(kt == 0), stop=(kt == 17))],
                     reads=[vm, PT, gconst], writes=[po, pd])
            rd = nxt(rd_)
            S.run("dve", lambda e, rd=rd, pd=pd, n=n: e.reciprocal(out=rd.ap[:, 0:n], in_=pd.ap[:, 0:n]), reads=[pd], writes=[rd])
            S.run("dve", lambda e, rd=rd, po=po, osb=osb, t0=t0, t1=t1, n=n: e.tensor_tensor(out=osb.ap[:, t0:t1], in0=po.ap[:, 0:n],
                                                                                         in1=rd.ap[:, 0:n], op=ALU.mult),
                  reads=[po, rd], writes=[osb])
        S.dma("sp", mixT_d[8 + h], osb.ap[:], osb.sem, reads=[osb])
    S.flush()
    L.close()
    if debug == "p4":
        return finish(nc, S, G)

    R = Scope()
    xT = R.sb("xT", [128, 16, TH], F32)
    xTb = [Buf(xT[:, c, :], sem=S.dsem("xTl")) for c in range(16)]
    hT = R.sb("hT", [128, 16, TH], BF16)
    hTb = [Buf(hT[:, c, :]) for c in range(16)]
    rsb = R.sb("rsb", [128, TH], F32)
    rsB = Buf(rsb)
    sqn_ = [Buf(R.sb("sqn%d" % i, [128, TH], BF16)) for i in range(2)]
    tmpn_ = [Buf(R.sb("tmpn%d" % i, [128, TH], F32)) for i in range(2)]
    for c in range(16):
        S.dma("sp", xT[:, c, :], xT_d[c], xTb[c].sem, writes=[xTb[c]])

    def dump_x():
        for c in range(16):
            S.dma("sp", xT_d[c], xT[:, c, :], xTb[c].sem, reads=[xTb[c]])
        S.flush()

    def token_rstd(ncols=TH):
        pss = [pf[3], pf[4], pf[5]]
        for c in range(16):
            sq = sqn_[c % 2]
            S.run("act", lambda e, c=c, sq=sq: e.activation(out=sq.ap[:], in_=xT[:, c, :], func=AF.Square), reads=[xTb[c]], writes=[sq])
            for gi, (t0, t1) in enumerate(TG):
                S.mm([lambda e, gi=gi, sq=sq, t0=t0, t1=t1, c=c: e.matmul(pss[gi].ap[:, 0:t1 - t0], lhsT=ones_b[:], rhs=sq.ap[:, t0:t1],
                                                                          start=(c == 0), stop=(c == 15))], reads=[sq, gconst], writes=[pss[gi]])
        for gi, (t0, t1) in enumerate(TG):
            S.run("dve", lambda e, gi=gi, t0=t0, t1=t1: e.tensor_scalar(out=rsb[:, t0:t1], in0=pss[gi].ap[:, 0:t1 - t0], scalar1=1.0 / D,
                                                                        scalar2=EPS, op0=ALU.mult, op1=ALU.add), reads=[pss[gi]], writes=[rsB])
        S.run("act", lambda e: e.activation(out=rsb[:], in_=rsb[:], func=AF.Sqrt), writes=[rsB])
        S.run("dve", lambda e: e.reciprocal(out=rsb[:], in_=rsb[:]), writes=[rsB])

    def prenorm(a_fn, sh_fn):
        token_rstd()
        for c in range(16):
            tm_ = tmpn_[c % 2]
            S.run("dve", lambda e, c=c, tm_=tm_: e.scalar_tensor_tensor(out=tm_.ap[:], in0=xT[:, c, :], scalar=a_fn(c), in1=rsb[:],
                                                                        op0=ALU.mult, op1=ALU.mult), reads=[xTb[c], rsB, gconst], writes=[tm_])
            S.run("act", lambda e, c=c, tm_=tm_: e.activation(out=hT[:, c, :], in_=tm_.ap[:], func=AF.Identity, bias=sh_fn(c)),
                  reads=[tm_, gconst], writes=[hTb[c]])

    L = Scope()
    mixT = L.sb("mixT", [128, 16, TH], BF16)
    mixB = Buf(mixT, sem=S.dsem("mixT"))
    S.dma("sp", mixT[:], mixT_d.rearrange("c p t -> p c t"), mixB.sem, writes=[mixB])
    wo_ = [Buf(L.sb("wo%d" % i, [128, 16, 512], BF16), sem=S.dsem("wo%d" % i)) for i in range(2)]
    w_out_v = w_out.rearrange("(c p) n -> p c n", p=128)
    for g4 in range(4):
        wo = wo_[g4 % 2]
        S.dma("pool", wo.ap[:], w_out_v[:, :, g4 * 512:(g4 + 1) * 512], wo.sem, writes=[wo])
        for dl in range(4):
            dc = g4 * 4 + dl
            for (t0, t1) in TG:
                pi = nxt(pf[0:3])
                S.mm([lambda e, c=c, pi=pi, wo=wo, dl=dl, t0=t0, t1=t1: e.matmul(pi.ap[:, 0:t1 - t0], lhsT=wo.ap[:, c, dl * 128:(dl + 1) * 128],
                                                                                 rhs=mixT[:, c, t0:t1], start=(c == 0), stop=(c == 15))
                      for c in range(16)], reads=[wo, mixB], writes=[pi])
                S.run("dve", lambda e, pi=pi, dc=dc, t0=t0, t1=t1: e.scalar_tensor_tensor(
                    out=xT[:, dc, t0:t1], in0=pi.ap[:, 0:t1 - t0], scalar=modT[:, 0, 32 + dc:33 + dc], in1=xT[:, dc, t0:t1],
                    op0=ALU.mult, op1=ALU.add), reads=[pi, gconst], writes=[xTb[dc]])
    S.flush()
    L.close()
    if debug == "p5":
        dump_x()
        R.close()
        return finish(nc, S, G)

    def ffn(l):
        prenorm(lambda c: a2[:, l, c:c + 1], lambda c: modT[:, l, 48 + c:49 + c])
        L = Scope()
        wu_ = [Buf(L.sb("wu%d" % i, [128, 16, 512], BF16), sem=S.dsem("wu%d" % i)) for i in range(2)]
        wd_ = [Buf(L.sb("wd%d" % i, [128, 2, 2048], BF16), sem=S.dsem("wd%d" % i)) for i in range(2)]
        act_ = [Buf(L.sb("actT%d" % i, [128, 2, TH], BF16)) for i in range(2)]
        ua_ = [Buf(L.sb("ua%d" % i, [128, 352], F32)) for i in range(2)]
        ug_ = [Buf(L.sb("ug%d" % i, [128, 352], F32)) for i in range(2)]
        ca_ = [Buf(L.sb("ca%d" % i, [128, 352], F32)) for i in range(2)]
        cg_ = [Buf(L.sb("cg%d" % i, [128, 352], F32)) for i in range(2)]
        wuv = w_up[l].rearrange("(c p) n -> p c n", p=128)
        wdv = w_down[l].rearrange("(j p) n -> p j n", p=128)
        uic = [0]

        def U(fb, dgen=None):
            wu, wd, actT = wu_[fb % 2], wd_[fb % 2], act_[fb % 2]
            S.dma("pool", wu.ap[:, :, 0:256], wuv[:, :, fb * 256:(fb + 1) * 256], wu.sem, writes=[wu])
            S.dma("pool", wu.ap[:, :, 256:512], wuv[:, :, FF + fb * 256:FF + (fb + 1) * 256], wu.sem, writes=[wu])
            S.dma("pool", wd.ap[:], wdv[:, fb * 2:fb * 2 + 2, :], wd.sem, writes=[wd])
            for j in range(2):
                jj = fb * 2 + j
                for gi, (t0, t1) in enumerate(TG):
                    lo, hi = max(t0 - 1, 0), min(t1 + 1, TH)
                    n, nn = t1 - t0, hi - lo
                    off = lo - (t0 - 1)
                    ua, ug, ca, cg = ua_[uic[0] % 2], ug_[uic[0] % 2], ca_[uic[0] % 2], cg_[uic[0] % 2]
                    uic[0] += 1
                    for (ub, col0, pp) in ((ua, j * 128, nxt(pf[0:2])), (ug, 256 + j * 128, nxt(pf[2:4]))):
                        S.mm([lambda e, c=c, pp=pp, wu=wu, col0=col0, lo=lo, hi=hi, nn=nn: e.matmul(
                            pp.ap[:, 0:nn], lhsT=wu.ap[:, c, col0:col0 + 128], rhs=hT[:, c, lo:hi], start=(c == 0), stop=(c == 15))
                            for c in range(16)], reads=[wu] + hTb, writes=[pp])
                        if off == 1:
                            S.run("dve", lambda e, ub=ub: e.memset(ub.ap[:, 0:1], 0.0), writes=[ub])
                        if hi == TH:
                            S.run("dve", lambda e, ub=ub, nn=nn, off=off: e.memset(ub.ap[:, off + nn:off + nn + 1], 0.0), writes=[ub])
                        S.run("act", lambda e, ub=ub, pp=pp, nn=nn, off=off: e.activation(out=ub.ap[:, off:off + nn], in_=pp.ap[:, 0:nn],
                                                                                       func=AF.Copy), reads=[pp], writes=[ub])
                        cb_, ch = (ca, jj) if ub is ua else (cg, 44 + jj)
                        S.run("act", lambda e, cb_=cb_, pp=pp, n=n, off=off, ch=ch: e.activation(
                            out=cb_.ap[:, 0:n], in_=pp.ap[:, 1 - off:1 - off + n], func=AF.Identity,
                            scale=convw_s[:, l, 1, ch:ch + 1], bias=convw_s[:, l, 3, ch:ch + 1]), reads=[pp, gconst], writes=[cb_])
                    for (ub, cb_, ch) in ((ua, ca, jj), (ug, cg, 44 + jj)):
                        S.run("dve", lambda e, ub=ub, cb_=cb_, ch=ch, n=n: e.scalar_tensor_tensor(
                            out=cb_.ap[:, 0:n], in0=ub.ap[:, 0:n], scalar=convw_s[:, l, 0, ch:ch + 1], in1=cb_.ap[:, 0:n],
                            op0=ALU.mult, op1=ALU.add), reads=[ub, gconst], writes=[cb_])
                        S.run("dve", lambda e, ub=ub, cb_=cb_, ch=ch, n=n: e.scalar_tensor_tensor(
                            out=cb_.ap[:, 0:n], in0=ub.ap[:, 2:n + 2], scalar=convw_s[:, l, 2, ch:ch + 1], in1=cb_.ap[:, 0:n],
                            op0=ALU.mult, op1=ALU.add), reads=[ub, gconst], writes=[cb_])
                    S.run("act", lambda e, cg=cg, n=n: e.activation(out=cg.ap[:, 0:n], in_=cg.ap[:, 0:n], func=AF.Silu), writes=[cg])
                    S.run("dve", lambda e, cg=cg, ca=ca, actT=actT, j=j, t0=t0, t1=t1, n=n: e.tensor_tensor(
                        out=actT.ap[:, j, t0:t1], in0=cg.ap[:, 0:n], in1=ca.ap[:, 0:n], op=ALU.mult), reads=[cg, ca], writes=[actT])
                    if dgen is not None:
                        for _ in range(8):
                            next(dgen, None)

        def Dp(fb):
            wd, actT = wd_[fb % 2], act_[fb % 2]
            for dc in range(16):
                for (t0, t1) in TG:
                    pi = nxt(pf[4:6])
                    S.mm([lambda e, j=j, pi=pi, wd=wd, dc=dc, actT=actT, t0=t0, t1=t1: e.matmul(
                        pi.ap[:, 0:t1 - t0], lhsT=wd.ap[:, j, dc * 128:(dc + 1) * 128], rhs=actT.ap[:, j, t0:t1], start=(j == 0), stop=(j == 1))
                        for j in range(2)], reads=[wd, actT], writes=[pi])
                    S.run("dve", lambda e, pi=pi, dc=dc, t0=t0, t1=t1: e.scalar_tensor_tensor(
                        out=xT[:, dc, t0:t1], in0=pi.ap[:, 0:t1 - t0], scalar=modT[:, l, 80 + dc:81 + dc], in1=xT[:, dc, t0:t1],
                        op0=ALU.mult, op1=ALU.add), reads=[pi, gconst], writes=[xTb[dc]])
                    yield

        U(0)
        for fb in range(22):
            dgen = Dp(fb)
            if fb + 1 < 22:
                U(fb + 1)
            for _ in dgen:
                pass
        S.flush()
        L.close()

    ffn(0)
    if debug == "p7":
        dump_x()
        R.close()
        return finish(nc, S, G)

    token_rstd()
    L = Scope()
    PW = 8 + TH + 16
    hp_ = [Buf(L.sb("hp%d" % i, [128, PW], F32)) for i in range(2)]
    wa_ = [Buf(L.sb("wa%d" % i, [128, PW], F32)) for i in range(2)]
    wb_ = [Buf(L.sb("wb%d" % i, [128, PW], F32)) for i in range(2)]
    t16_ = [Buf(L.sb("t16_%d" % i, [128, 16], F32)) for i in range(2)]
    pw_ = [Buf(L.sb("pw%d" % i, [128, 4, 512], BF16), sem=S.dsem("pw%d" % i)) for i in range(2)]
    for hb in hp_:
        S.run("dve", lambda e, hb=hb: e.memset(hb.ap[:], 0.0), writes=[hb])
    for g in range(4):
        w = 2 << g
        pw = pw_[g % 2]
        S.dma("pool", pw.ap[:], pool_w[g].rearrange("(c p) n -> p c n", p=128), pw.sem, writes=[pw])
        for cc in range(4):
            c = 4 * g + cc
            hb, wa, wb2, t16 = hp_[c % 2], wa_[c % 2], wb_[c % 2], t16_[c % 2]
            S.run("dve", lambda e, c=c, hb=hb: e.scalar_tensor_tensor(out=hb.ap[:, 8:8 + TH], in0=xT[:, c, :], scalar=a1[:, 1, c:c + 1],
                                                                      in1=rsb[:], op0=ALU.mult, op1=ALU.mult),
                  reads=[xTb[c], rsB, gconst], writes=[hb])
            S.run("act", lambda e, c=c, hb=hb: e.activation(out=hb.ap[:, 8:8 + TH], in_=hb.ap[:, 8:8 + TH], func=AF.Identity,
                                                            bias=modT[:, 1, c:c + 1]), reads=[gconst], writes=[hb])
            src, width, k = hb, PW, 1
            bufs = [wa, wb2]
            bi = 0
            while k < w:
                dst = bufs[bi % 2]
                bi += 1
                nw = width - k
                S.run("dve", lambda e, src=src, dst=dst, nw=nw, k=k: e.tensor_tensor(out=dst.ap[:, 0:nw], in0=src.ap[:, 0:nw],
                                                                                   in1=src.ap[:, k:k + nw], op=ALU.add),
                      reads=[src], writes=[dst])
                src, width, k = dst, nw, 2 * k
            s0 = 8 - w // 2
            oth = bufs[bi % 2]
            S.run("dve", lambda e, src=src, oth=oth, s0=s0: e.tensor_tensor(out=oth.ap[:, 0:TH], in0=src.ap[:, s0 + 1:s0 + 1 + TH],
                                                                          in1=src.ap[:, s0:s0 + TH], op=ALU.subtract),
                  reads=[src], writes=[oth])
            S.run("dve", lambda e, src=src, oth=oth, s0=s0: e.scalar_tensor_tensor(out=oth.ap[:, 0:TH], in0=oth.ap[:, 0:TH], scalar=flip_s[:, 0:1],
                                                                                 in1=src.ap[:, s0:s0 + TH], op0=ALU.mult, op1=ALU.add),
                  reads=[src, gconst], writes=[oth])
            S.run("dve", lambda e, oth=oth, g=g, t16=t16: e.tensor_tensor(out=t16.ap[:], in0=oth.ap[:, 0:16], in1=pconst_s[:, g, :], op=ALU.mult),
                  reads=[oth, gconst], writes=[t16])
            S.run("dve", lambda e, oth=oth, hb=hb, c=c, w=w: e.scalar_tensor_tensor(out=hT[:, c, :], in0=oth.ap[:, 0:TH], scalar=1.0 / w,
                                                                                  in1=hb.ap[:, 8:8 + TH], op0=ALU.mult, op1=ALU.subtract),
                  reads=[oth, hb], writes=[hTb[c]])
            S.run("dve", lambda e, hb=hb, c=c, t16=t16: e.tensor_tensor(out=hT[:, c, 0:16], in0=t16.ap[:], in1=hb.ap[:, 8:24], op=ALU.subtract),
                  reads=[t16, hb], writes=[hTb[c]])
        for dl in range(4):
            dc = 4 * g + dl
            for (t0, t1) in TG:
                pi = nxt(pf[0:3])
                S.mm([lambda e, cc=cc, pi=pi, pw=pw, dl=dl, g=g, t0=t0, t1=t1: e.matmul(
                    pi.ap[:, 0:t1 - t0], lhsT=pw.ap[:, cc, dl * 128:(dl + 1) * 128], rhs=hT[:, 4 * g + cc, t0:t1], start=(cc == 0), stop=(cc == 3))
                    for cc in range(4)], reads=[pw] + hTb[4 * g:4 * g + 4], writes=[pi])
                S.run("dve", lambda e, pi=pi, dc=dc, t0=t0, t1=t1: e.scalar_tensor_tensor(
                    out=xT[:, dc, t0:t1], in0=pi.ap[:, 0:t1 - t0], scalar=gps[:, dc:dc + 1], in1=xT[:, dc, t0:t1],
                    op0=ALU.mult, op1=ALU.add), reads=[pi, gconst], writes=[xTb[dc]])
    S.flush()
    L.close()
    if debug == "p8":
        dump_x()
        R.close()
        return finish(nc, S, G)

    ffn(1)

    token_rstd()
    L = Scope()
    yT = L.sb("yT", [128, 16, TO], F32)
    yTb = [Buf(yT[:, c, :]) for c in range(16)]
    ot_ = [Buf(L.sb("ot%d" % i, [128, D], F32), sem=S.dsem("ot%d" % i)) for i in range(2)]
    for c in range(16):
        S.run("dve", lambda e, c=c: e.scalar_tensor_tensor(out=yT[:, c, :], in0=xT[:, c, 0:TO], scalar=vecs_s[:, 4, c:c + 1], in1=rsb[:, 0:TO],
                                                           op0=ALU.mult, op1=ALU.mult), reads=[xTb[c], rsB, gconst], writes=[yTb[c]])
    for i in range(8):
        ot = ot_[i % 2]
        for c in range(16):
            pi = nxt(pf[0:6])
            S.mm([lambda e, c=c, i=i, pi=pi: e.transpose(out=pi.ap[:, 0:128], in_=yT[:, c, i * 128:(i + 1) * 128], identity=ident_f[:])],
                 reads=[yTb[c], gconst], writes=[pi])
            S.run("act" if c % 2 == 0 else "dve", (lambda e, c=c, pi=pi, ot=ot: e.activation(out=ot.ap[:, c * 128:(c + 1) * 128], in_=pi.ap[:, 0:128],
                                                                                        func=AF.Copy)) if c % 2 == 0 else
                  (lambda e, c=c, pi=pi, ot=ot: e.tensor_copy(out=ot.ap[:, c * 128:(c + 1) * 128], in_=pi.ap[:, 0:128])),
                  reads=[pi], writes=[ot])
        S.dma("sp", out_d[i * 128:(i + 1) * 128, :], ot.ap[:], ot.sem, reads=[ot])
    S.flush()
    L.close()
    R.close()
    return finish(nc, S, G)


def finish(nc, S, G):
    G.close()
    S.close()
    return nc


def _pp(v):
    v = np.asarray(v, np.float32)
    return np.ascontiguousarray(v.reshape(-1, 128).T)


def _rope_tab(flip, dim):
    j = np.arange(SEQ)
    t = (SEQ - 1 - j) if flip else j
    row = (t // 64).astype(np.float32)
    col = (t % 64).astype(np.float32)
    nf = dim // 4
    inv = (10000.0 ** (-np.arange(nf, dtype=np.float32) / nf)).astype(np.float32)
    ang = np.concatenate([row[:, None] * inv, col[:, None] * inv], -1)
    cs = np.stack([np.cos(ang), np.sin(ang)], 1).astype(np.float32)
    return np.ascontiguousarray(cs.reshape(16, 128, 2, dim // 2).transpose(1, 0, 2, 3))


def prepare_inputs(inp, k):
    b, half = k // 2, k % 2
    flip = half == 1
    x = inp["x"][b]
    ctx = inp["ctx"][b]
    if flip:
        x = x[::-1]
        ctx = ctx[::-1]
    m = {}
    m["x"] = np.ascontiguousarray(x, np.float32)
    m["ctx"] = np.ascontiguousarray(ctx, np.float32)
    m["cvec"] = np.ascontiguousarray(np.stack([_pp(inp["c"][b]), _pp(inp["c_ctx"])], -1))
    m["ada_w"] = inp["ada_w"]
    m["ada_b"] = np.ascontiguousarray(np.stack([_pp(inp["ada_b"][0]), _pp(inp["ada_b"][1])], 1))
    m["vecs"] = np.ascontiguousarray(np.stack([_pp(inp["norm1_g"][0]), _pp(inp["norm1_g"][1]), _pp(inp["norm2_g"][0]),
                                               _pp(inp["norm2_g"][1]), _pp(inp["final_g"])], 1))
    m["pvec"] = _pp(inp["pool_scale"][0])
    m["mlag"] = np.ascontiguousarray(np.concatenate([_pp(inp["mla_q_norm_g"][0]), _pp(inp["mla_kv_norm_g"][0])], 1))
    cw = np.zeros((128, 2, 4, 88), np.float32)
    for l in range(2):
        w3 = inp["ffn_conv_w"][l]
        if flip:
            w3 = w3[::-1]
        for t in range(3):
            cw[:, l, t, :] = _pp(w3[t])
        cw[:, l, 3, :] = _pp(inp["ffn_conv_b"][l])
    m["convw"] = cw
    m["w_up"] = inp["ffn_w_up"]
    m["w_down"] = inp["ffn_w_down"]
    m["w_in"] = inp["mix_w_in"][0]
    m["w_uq"] = inp["mla_w_uq"][0]
    m["w_ukv"] = inp["mla_w_ukv"][0]
    m["w_out"] = inp["mix_w_out"][0]
    m["pool_w"] = inp["pool_w"][0]
    df, db = inp["ret_decay_f"][0], inp["ret_decay_b"][0]
    d12 = np.stack([db, df], 0) if flip else np.stack([df, db], 0)
    m["dec"] = np.ascontiguousarray(np.broadcast_to(d12[None], (128, 2, 8)), np.float32)
    p = np.arange(128, dtype=np.float32)
    m["epos"] = np.ascontiguousarray(np.stack([p + 1, -(p + 1), 128 - p, -(128 - p)], 1), np.float32)
    mi, ni = np.meshgrid(np.arange(128), np.arange(128), indexing="ij")
    strict1, strict2 = (True, False) if flip else (False, True)
    mk1 = (ni > mi) if strict1 else (ni >= mi)
    mk2 = (ni < mi) if strict2 else (ni <= mi)
    m["masks"] = np.ascontiguousarray(np.stack([mk1, mk2], 1).astype(np.float32))
    m["ropeR"] = _rope_tab(flip, 128)
    m["ropeM"] = _rope_tab(flip, 64)
    j = np.arange(16)
    t = (SEQ - 1 - j) if flip else j
    pc = np.zeros((4, 16), np.float32)
    for gi, w in enumerate((2, 4, 8, 16)):
        lo = np.clip(t - w // 2, 0, SEQ)
        hi = np.clip(t - w // 2 + w, 0, SEQ)
        pc[gi] = 1.0 / (hi - lo)
    m["pconst"] = np.ascontiguousarray(np.broadcast_to(pc[None], (128, 4, 16)), np.float32)
    m["flipd"] = np.full((128, 1), 1.0 if flip else 0.0, np.float32)
    return m


_NC_CACHE = {}


def kernel(**inputs):
    inp = {k: np.asarray(v) for k, v in inputs.items()}
    if "nc" not in _NC_CACHE:
        _NC_CACHE["nc"] = build_program()
    nc = _NC_CACHE["nc"]
    in_maps = [prepare_inputs(inp, k) for k in range(8)]
    res = run_bass_kernel_spmd(nc, in_maps, core_ids=list(range(8)))
    out = np.zeros((4, SEQ, D), np.float32)
    for k in range(8):
        o = np.asarray(res.results[k]["out"], np.float32)
        b, half = k // 2, k % 2
        if half == 0:
            out[b, :TO] = o
        else:
            out[b, TO:] = o[::-1]
    return out
```
